# Optimizing a Trainium2 kernel written in Bass

```python
import jax, jax.numpy as jnp
from jax import lax
import numpy as np

D_MODEL = 1024
BATCH = 8
SEQ = 4096
DEPTH = 1
DEC_BATCH = 8
DEC_SEQ = 16
PAST_LEN = 1024

CHUNK = 64
N_META = 16
HEAD_DIM = 64
N_Q_HEADS = 16
N_KV_HEADS = 2
GQA_GROUP = N_Q_HEADS // N_KV_HEADS
WINDOW = 128
WIN_CHUNKS = WINDOW // CHUNK
ROPE_THETA = 10000.0
POOL_WINDOWS = (2, 4, 8, 16)
N_POOL_GROUPS = 4
POOL_WIDTH = D_MODEL // 2
POOL_GROUP_W = POOL_WIDTH // N_POOL_GROUPS
POOL_HIST = max(POOL_WINDOWS) - 1
Q_W = N_Q_HEADS * HEAD_DIM
KV_W = N_KV_HEADS * HEAD_DIM
GATE_W = 2 * D_MODEL
IN_W = Q_W + 2 * KV_W + POOL_WIDTH + GATE_W
D_FF = ((8 * D_MODEL // 3) + 127) // 128 * 128
CONV_W = 3
RMS_EPS = 1e-6

kernel_name = "hybrid_swa_pool_convffn_stream_step"


def rmsnorm(x, g):
    xf = x.astype(jnp.float32)
    r = lax.rsqrt(jnp.mean(xf * xf, axis=-1, keepdims=True) + RMS_EPS)
    return (xf * r).astype(x.dtype) * g


def rope(x, pos):
    half = HEAD_DIM // 2
    inv = ROPE_THETA ** (-jnp.arange(half, dtype=jnp.float32) / half)
    ang = pos.astype(jnp.float32)[:, None] * inv[None, :]
    cos = jnp.cos(ang)[None, :, None, :]
    sin = jnp.sin(ang)[None, :, None, :]
    xf = x.astype(jnp.float32)
    x1, x2 = xf[..., :half], xf[..., half:]
    return jnp.concatenate([x1 * cos - x2 * sin, x2 * cos + x1 * sin], axis=-1).astype(x.dtype)


def attend(q, k, v, sinks, mask):
    s = jnp.einsum('...qngd,...snd->...ngqs', q, k).astype(jnp.float32) * (HEAD_DIM ** -0.5)
    if mask is not None:
        s = jnp.where(mask, s, -1e30)
    sink = jnp.broadcast_to(sinks.astype(jnp.float32)[..., None, None], s.shape[:-1] + (1,))
    p = jax.nn.softmax(jnp.concatenate([s, sink], axis=-1), axis=-1)[..., :-1]
    return jnp.einsum('...ngqs,...snd->...qngd', p.astype(v.dtype), v)


def project_inputs(h, w_in, b_gate):
    z = h @ w_in
    q = z[..., :Q_W]
    k = z[..., Q_W:Q_W + KV_W]
    v = z[..., Q_W + KV_W:Q_W + 2 * KV_W]
    p = z[..., Q_W + 2 * KV_W:Q_W + 2 * KV_W + POOL_WIDTH]
    g = jax.nn.sigmoid((z[..., Q_W + 2 * KV_W + POOL_WIDTH:] + b_gate).astype(jnp.float32)).astype(h.dtype)
    return q, k, v, p, g[..., :D_MODEL], g[..., D_MODEL:]


def multiscale_pool(p, pos):
    bsz, s = p.shape[0], p.shape[1]
    pg = p.reshape(bsz, s, N_POOL_GROUPS, POOL_GROUP_W).astype(jnp.float32)
    cs = jnp.cumsum(pg, axis=1)
    cs0 = jnp.concatenate([jnp.zeros_like(cs[:, :1]), cs], axis=1)
    outs = []
    for gi, w in enumerate(POOL_WINDOWS):
        c = cs0[:, :, gi]
        prev = jnp.concatenate([jnp.zeros_like(c[:, :w - 1]), c[:, :s - w + 1]], axis=1)
        cnt = jnp.minimum(w, pos + 1).astype(jnp.float32)[None, :, None]
        outs.append((c[:, 1:] - prev) / cnt - pg[:, :, gi])
    return jnp.stack(outs, axis=2)


def mixer_residual(x, o_attn, pm, g_attn, g_pool, w_attn_o, w_pool_grp, pool_scale, w_pool_o, w_out):
    bsz, s = pm.shape[0], pm.shape[1]
    pool = jnp.einsum('bsgc,gcd->bsgd', pm.astype(x.dtype), w_pool_grp).reshape(bsz, s, POOL_WIDTH) * pool_scale
    mix = g_attn * (o_attn @ w_attn_o) + g_pool * (pool @ w_pool_o)
    return x + mix @ w_out


def channel_mixer(x1, conv_prefix, g_norm_ffn, w_up, conv_w, conv_b, w_down):
    up = rmsnorm(x1, g_norm_ffn) @ w_up
    up_ext = jnp.concatenate([conv_prefix, up], axis=1)
    t = up.shape[1]
    c = conv_b + up_ext[:, 0:t] * conv_w[0]
    for j in range(1, CONV_W):
        c = c + up_ext[:, j:j + t] * conv_w[j]
    gate, val = c[..., :D_FF], c[..., D_FF:]
    return x1 + (jax.nn.silu(gate) * val) @ w_down, up_ext[:, -(CONV_W - 1):]


def setup_inputs(seed: int = 0) -> dict:
    key = jax.random.key(seed)
    ks = jax.random.split(key, 24)
    n = jax.random.normal
    f32 = jnp.float32
    return {
        "x_prompt": n(ks[0], (BATCH, SEQ, D_MODEL), f32),
        "x_sample": n(ks[1], (DEC_BATCH, DEC_SEQ, D_MODEL), f32),
        "cache_swa_k": n(ks[2], (DEC_BATCH, WINDOW, N_KV_HEADS, HEAD_DIM), f32),
        "cache_swa_v": n(ks[3], (DEC_BATCH, WINDOW, N_KV_HEADS, HEAD_DIM), f32),
        "cache_meta_k": n(ks[4], (DEC_BATCH, N_META, N_KV_HEADS, HEAD_DIM), f32),
        "cache_meta_v": n(ks[5], (DEC_BATCH, N_META, N_KV_HEADS, HEAD_DIM), f32),
        "state_pool": n(ks[6], (DEC_BATCH, POOL_HIST, POOL_WIDTH), f32),
        "state_conv": n(ks[7], (DEC_BATCH, CONV_W - 1, 2 * D_FF), f32),
        "meta_tokens": n(ks[8], (N_META, D_MODEL), f32),
        "g_norm_mix": 1.0 + 0.05 * n(ks[9], (D_MODEL,), f32),
        "w_in": n(ks[10], (D_MODEL, IN_W), f32) * D_MODEL ** -0.5,
        "b_gate": 0.01 * n(ks[11], (GATE_W,), f32),
        "sinks": 0.5 * n(ks[12], (N_Q_HEADS,), f32),
        "w_attn_o": n(ks[13], (Q_W, D_MODEL), f32) * Q_W ** -0.5,
        "w_pool_grp": n(ks[14], (N_POOL_GROUPS, POOL_GROUP_W, POOL_GROUP_W), f32) * POOL_GROUP_W ** -0.5,
        "pool_scale": 1.0 + 0.1 * n(ks[15], (POOL_WIDTH,), f32),
        "w_pool_o": n(ks[16], (POOL_WIDTH, D_MODEL), f32) * POOL_WIDTH ** -0.5,
        "w_out": n(ks[17], (D_MODEL, D_MODEL), f32) * D_MODEL ** -0.5,
        "g_norm_ffn": 1.0 + 0.05 * n(ks[18], (D_MODEL,), f32),
        "w_up": n(ks[19], (D_MODEL, 2 * D_FF), f32) * D_MODEL ** -0.5,
        "conv_w": n(ks[20], (CONV_W, 2 * D_FF), f32) * CONV_W ** -0.5,
        "conv_b": 0.01 * n(ks[21], (2 * D_FF,), f32),
        "w_down": n(ks[22], (D_FF, D_MODEL), f32) * D_FF ** -0.5,
        "g_norm_final": 1.0 + 0.05 * n(ks[23], (D_MODEL,), f32),
    }


def reference(x_prompt, x_sample, cache_swa_k, cache_swa_v, cache_meta_k, cache_meta_v, state_pool, state_conv,
              meta_tokens, g_norm_mix, w_in, b_gate, sinks, w_attn_o, w_pool_grp, pool_scale, w_pool_o, w_out,
              g_norm_ffn, w_up, conv_w, conv_b, w_down, g_norm_final):
    dt = x_prompt.dtype
    sinks_r = sinks.reshape(N_KV_HEADS, GQA_GROUP)

    bsz, seq = x_prompt.shape[0], x_prompt.shape[1]
    length = N_META + seq
    nc = seq // CHUNK
    x = jnp.concatenate([jnp.broadcast_to(meta_tokens.astype(dt)[None], (bsz, N_META, D_MODEL)), x_prompt], axis=1)
    pos = jnp.arange(length, dtype=jnp.int32)
    h = rmsnorm(x, g_norm_mix)
    q, k, v, p, g_attn, g_pool = project_inputs(h, w_in, b_gate)
    q = rope(q.reshape(bsz, length, N_Q_HEADS, HEAD_DIM), pos).reshape(bsz, length, N_KV_HEADS, GQA_GROUP, HEAD_DIM)
    k = rope(k.reshape(bsz, length, N_KV_HEADS, HEAD_DIM), pos)
    v = v.reshape(bsz, length, N_KV_HEADS, HEAD_DIM)
    km, vm = k[:, :N_META], v[:, :N_META]
    o_meta = attend(q[:, :N_META], km, vm, sinks_r, None)
    qr = q[:, N_META:].reshape(bsz, nc, CHUNK, N_KV_HEADS, GQA_GROUP, HEAD_DIM)

    def band(t):
        tc = t.reshape(bsz, nc, CHUNK, N_KV_HEADS, HEAD_DIM)
        tp = jnp.pad(tc, ((0, 0), (WIN_CHUNKS, 0), (0, 0), (0, 0), (0, 0)))
        return jnp.concatenate([tp[:, j:j + nc] for j in range(WIN_CHUNKS + 1)], axis=2)

    meta_b = lambda t: jnp.broadcast_to(t[:, None], (bsz, nc, N_META, N_KV_HEADS, HEAD_DIM))
    kb = jnp.concatenate([meta_b(km), band(k[:, N_META:])], axis=2)
    vb = jnp.concatenate([meta_b(vm), band(v[:, N_META:])], axis=2)
    cidx = jnp.arange(nc)[:, None]
    jidx = jnp.arange((WIN_CHUNKS + 1) * CHUNK)[None, :]
    band_ok = (cidx - WIN_CHUNKS + jidx // CHUNK) >= 0
    mask = jnp.concatenate([jnp.ones((nc, N_META), dtype=bool), band_ok], axis=1)[:, None, None, None, :]
    o_real = attend(qr, kb, vb, sinks_r, mask)
    o_attn = jnp.concatenate([o_meta.reshape(bsz, N_META, Q_W), o_real.reshape(bsz, seq, Q_W)], axis=1)
    pm = multiscale_pool(p, pos)
    x1 = mixer_residual(x, o_attn, pm, g_attn, g_pool, w_attn_o, w_pool_grp, pool_scale, w_pool_o, w_out)
    conv_prefix = jnp.zeros((bsz, CONV_W - 1, 2 * D_FF), dtype=dt)
    x2, p_conv = channel_mixer(x1, conv_prefix, g_norm_ffn, w_up, conv_w, conv_b, w_down)
    y_prompt = rmsnorm(x2, g_norm_final)[:, N_META:]
    p_swa_k = k[:, -WINDOW:]
    p_swa_v = v[:, -WINDOW:]
    p_pool = p[:, -POOL_HIST:]

    dbsz, t = x_sample.shape[0], x_sample.shape[1]
    pos_s = N_META + PAST_LEN + jnp.arange(t, dtype=jnp.int32)
    hs = rmsnorm(x_sample, g_norm_mix)
    qs, ks_, vs, ps, ga_s, gp_s = project_inputs(hs, w_in, b_gate)
    qs = rope(qs.reshape(dbsz, t, N_Q_HEADS, HEAD_DIM), pos_s).reshape(dbsz, t, N_KV_HEADS, GQA_GROUP, HEAD_DIM)
    ks_ = rope(ks_.reshape(dbsz, t, N_KV_HEADS, HEAD_DIM), pos_s)
    vs = vs.reshape(dbsz, t, N_KV_HEADS, HEAD_DIM)
    k_all = jnp.concatenate([cache_meta_k, cache_swa_k, ks_], axis=1)
    v_all = jnp.concatenate([cache_meta_v, cache_swa_v, vs], axis=1)
    o_s = attend(qs, k_all, v_all, sinks_r, None).reshape(dbsz, t, Q_W)
    p_ext = jnp.concatenate([state_pool, ps], axis=1)
    pos_ext = N_META + PAST_LEN - POOL_HIST + jnp.arange(POOL_HIST + t, dtype=jnp.int32)
    pm_s = multiscale_pool(p_ext, pos_ext)[:, POOL_HIST:]
    x1s = mixer_residual(x_sample, o_s, pm_s, ga_s, gp_s, w_attn_o, w_pool_grp, pool_scale, w_pool_o, w_out)
    x2s, s_conv = channel_mixer(x1s, state_conv, g_norm_ffn, w_up, conv_w, conv_b, w_down)
    y_sample = rmsnorm(x2s, g_norm_final)
    s_pool = p_ext[:, -POOL_HIST:]

    return (y_prompt, y_sample, p_swa_k, p_swa_v, km, vm, p_pool, p_conv, ks_, vs, s_pool, s_conv)
```

```python
from contextlib import ExitStack
import math
import numpy as np
import concourse.bass as bass
import concourse.mybir as mybir
from concourse.bass_utils import run_bass_kernel_spmd
F32 = mybir.dt.float32
BF16 = mybir.dt.bfloat16
I32 = mybir.dt.int32
AF = mybir.ActivationFunctionType
ALU = mybir.AluOpType
ENGS = ('pe', 'act', 'dve', 'pool', 'sp')

class Res:
    __slots__ = ('name', 'w', 'r')

    def __init__(self, name):
        self.name = name
        self.w = None
        self.r = []

class Op:
    __slots__ = ('id', 'eng', 'fn', 'deps', 'is_dma', 'signal', 'ticket', 'dsem', 'dval', 'prewait', 'out')

class Sched:
    def __init__(self, nc, n_dma_sems=16):
        self.nc = nc
        self.ops = []
        self.eng_ops = {e: [] for e in ENGS}
        self.n_dma_sems = n_dma_sems
        self.n_dma_sems_eng = {'pool': 64}
        self.dma_count = {e: 0 for e in ENGS}
        self.out_dmas = []

    def op(self, eng, fn, reads=(), writes=(), dma=False, out=False, force=()):
        o = Op()
        o.id = len(self.ops)
        o.eng = eng
        o.fn = fn
        o.is_dma = dma
        o.signal = False
        o.ticket = None
        o.out = out
        o.prewait = None
        deps = set()
        for r in reads:
            if r.w is not None:
                deps.add(r.w)
        for w in writes:
            if w.w is not None:
                deps.add(w.w)
            deps.update(w.r)
        for r in reads:
            r.r.append(o.id)
        for w in writes:
            w.w = o.id
            w.r = []
        deps.discard(o.id)
        if eng == 'pe' and (not dma):
            deps = {d for d in deps if not (self.ops[d].eng == 'pe' and (not self.ops[d].is_dma))}
        deps |= set(force)
        o.deps = deps
        if dma:
            k = self.dma_count[eng]
            self.dma_count[eng] += 1
            nds = self.n_dma_sems_eng.get(eng, self.n_dma_sems)
            o.dsem = (eng, k % nds)
            o.dval = 16 * (k // nds + 1)
            if k >= nds:
                o.prewait = (o.dsem, o.dval - 16)
            if out:
                self.out_dmas.append(o.id)
        self.ops.append(o)
        self.eng_ops[eng].append(o)
        return o

    def finish(self):
        o = Op()
        o.id = len(self.ops)
        o.eng = 'sp'
        o.fn = None
        o.is_dma = False
        o.signal = False
        o.ticket = None
        o.out = False
        o.prewait = None
        o.deps = set(self.out_dmas)
        self.ops.append(o)
        self.eng_ops['sp'].append(o)

    def emit(self):
        nc = self.nc
        pos = {}
        for e in ENGS:
            for i, o in enumerate(self.eng_ops[e]):
                pos[o.id] = i
        for o in self.ops:
            best = {}
            dmas = set()
            for d in o.deps:
                od = self.ops[d]
                if od.is_dma:
                    dmas.add(d)
                elif od.eng not in best or pos[d] > pos[best[od.eng]]:
                    best[od.eng] = d
            for d in best.values():
                self.ops[d].signal = True
            o.deps = dmas | set(best.values())
        for e in ENGS:
            t = 0
            for o in self.eng_ops[e]:
                if o.signal:
                    t += 1
                    o.ticket = t
        with ExitStack() as st:
            esem = {e: st.enter_context(nc.semaphore('es_' + e)) for e in ENGS}
            dsem = {}
            for e in ENGS:
                if self.dma_count[e] > 0:
                    for i in range(min(self.n_dma_sems_eng.get(e, self.n_dma_sems), self.dma_count[e])):
                        dsem[e, i] = st.enter_context(nc.semaphore('ds_%s_%d' % (e, i)))
            block = st.enter_context(nc.Block())

            def mk(e):

                def body(eng):
                    known = {}

                    def wait(key, sem, val):
                        if known.get(key, 0) < val:
                            eng.wait_ge(sem, val)
                            known[key] = val
                    for o in self.eng_ops[e]:
                        if o.prewait is not None:
                            wait(o.prewait[0], dsem[o.prewait[0]], o.prewait[1])
                        ws = {}
                        for d in o.deps:
                            od = self.ops[d]
                            if od.is_dma:
                                key, val = (od.dsem, od.dval)
                            else:
                                key, val = (od.eng, od.ticket)
                            if ws.get(key, 0) < val:
                                ws[key] = val
                        for key, val in ws.items():
                            sem = dsem[key] if isinstance(key, tuple) else esem[key]
                            wait(key, sem, val)
                        if o.fn is None:
                            continue
                        ins = o.fn(eng)
                        if o.is_dma:
                            ins.then_inc(dsem[o.dsem], 16)
                        elif o.signal:
                            ins.then_inc(esem[e], 1)
                return body
            block.tensor(mk('pe'))
            block.scalar(mk('act'))
            block.vector(mk('dve'))
            block.gpsimd(mk('pool'))
            block.sync(mk('sp'))
D = 1024
SEQ = 4096
NTILE = 8
TT = 512
NMETA = 16
DEC = 16
DFF = 2816
NCH = 44
NSLOT = 8
POOLW = (2, 4, 8, 16)
EPS = 1e-06

def build():
    nc = bass.Bass('TRN2', target_bir_lowering=False)
    S = Sched(nc)

    def din(name, shape):
        return nc.dram_tensor(name, list(shape), F32, kind='ExternalInput').ap()

    def dout(name, shape):
        return nc.dram_tensor(name, list(shape), F32, kind='ExternalOutput').ap()
    xp = din('xp', [SEQ, D])
    xs = din('xs', [DEC, D])
    cswak = din('cswak', [128, 128])
    cswav = din('cswav', [128, 128])
    cmetak = din('cmetak', [16, 128])
    cmetav = din('cmetav', [16, 128])
    spool = din('spool', [15, 512])
    sconv = din('sconv', [2, 2 * DFF])
    meta = din('meta', [NMETA, D])
    g1 = din('g1', [D])
    w_in = din('w_in', [D, 3840])
    b_gate = din('b_gate', [2048])
    sinks = din('sinks', [16])
    w_ao = din('w_ao', [D, D])
    w_grp = din('w_grp', [512, 128])
    pscale = din('pscale', [512])
    w_po = din('w_po', [512, D])
    w_out = din('w_out', [D, D])
    g2 = din('g2', [D])
    w_up = din('w_up', [D, 2 * DFF])
    conv_w = din('conv_w', [3, 2 * DFF])
    conv_b = din('conv_b', [2 * DFF])
    w_down = din('w_down', [DFF, D])
    g3 = din('g3', [D])
    ropec = din('ropec', [64])
    y_p = dout('y_p', [SEQ, D])
    y_s = dout('y_s', [DEC, D])
    o_swak = dout('o_swak', [128, 128])
    o_swav = dout('o_swav', [128, 128])
    o_metak = dout('o_metak', [16, 128])
    o_metav = dout('o_metav', [16, 128])
    o_ppool = dout('o_ppool', [15, 512])
    o_pconv = dout('o_pconv', [2, 2 * DFF])
    o_sk = dout('o_sk', [16, 128])
    o_sv = dout('o_sv', [16, 128])
    o_spool = dout('o_spool', [15, 512])
    o_sconv = dout('o_sconv', [2, 2 * DFF])

    def scr(name, shape):
        return nc.dram_tensor(name, list(shape), BF16, kind='Internal').ap()
    s_win = scr('s_win', [D, 3840])
    s_ao = scr('s_ao', [D, D])
    s_grp = scr('s_grp', [512, 128])
    s_po = scr('s_po', [512, D])
    s_out = scr('s_out', [D, D])
    s_up = scr('s_up', [D, 2 * DFF])
    s_down = scr('s_down', [DFF, D])

    def sb(name, shape, dt=F32):
        return nc.alloc_sbuf_tensor(name, list(shape), dt)
    identF = sb('identF', [128, 128])
    ident = sb('ident', [128, 128], BF16)
    identf = sb('identf', [128, 128])
    ones = sb('ones', [128, 128], BF16)
    g1col = sb('g1col', [128, 8])
    g2col = sb('g2col', [128, 8])
    g3b = sb('g3b', [128, D])
    bgcol = sb('bgcol', [128, 16])
    pscol = sb('pscol', [128, 4])
    cwcol = sb('cwcol', [128, 3, NCH])
    cbcol = sb('cbcol', [128, NCH])
    ropecb = sb('ropecb', [128, 64])
    sk_f = sb('sk_f', [1, 16])
    sk_e = sb('sk_e', [1, 16])
    esrow = sb('esrow', [1, 2, 512], BF16)
    invcnt = sb('invcnt', [128, 4, 16])
    posf = sb('posf', [128, 4])
    rp_u = sb('rp_u', [128, 4, 32])
    rp_i = sb('rp_i', [128, 4, 32], I32)
    rp_f = sb('rp_f', [128, 4, 32])
    rp_r = sb('rp_r', [128, 4, 32])
    cos_t = sb('cos_t', [128, 4, 32])
    sin_t = sb('sin_t', [128, 4, 32])
    nsin_t = sb('nsin_t', [128, 4, 32])
    wslot = [sb('wslot%d' % i, [128, 2048], BF16) for i in range(NSLOT)]
    NX = 8
    x_sb = [sb('x_sb%d' % i, [128, D]) for i in range(NX)]
    hb = [sb('hb%d' % i, [128, D], BF16) for i in range(2)]
    st_ss = [sb('st_ss%d' % i, [128, 4]) for i in range(2)]
    hT = sb('hT', [128, 8, TT], BF16)
    h2T = sb('h2T', [128, 8, TT], BF16)
    h2Ts = sb('h2Ts', [128, 8, 32], BF16)
    kf = [sb('kf%d' % i, [128, 128]) for i in range(2)]
    kbt = [sb('kbt%d' % i, [128, 128], BF16) for i in range(4)]
    vf = [sb('vf%d' % i, [128, 128]) for i in range(2)]
    QO = sb('QO', [128, 8, TT], BF16)
    KT = sb('KT', [128, 16 + 1024], BF16)
    VR = sb('VR', [128, 8, 256], BF16)
    VmLo = sb('VmLo', [128, 256], BF16)
    VmHi = sb('VmHi', [128, 256], BF16)
    KTs = sb('KTs', [128, 16], BF16)
    KTcm = sb('KTcm', [128, 16], BF16)
    KTcs = sb('KTcs', [128, 128], BF16)
    Vs = sb('Vs', [128, 256], BF16)
    Vcm = sb('Vcm', [128, 256], BF16)
    Vcs = sb('Vcs', [128, 256], BF16)
    cst_f = sb('cst_f', [128, 128])
    cst_b = sb('cst_b', [128, 128], BF16)
    pT = sb('pT', [128, 4, 15 + TT])
    pS = [sb('pS%d' % i, [128, 4, 31]) for i in range(2)]
    ptmp = sb('ptmp', [128, 16])
    pmT = sb('pmT', [128, 4, TT], BF16)
    poolT = sb('poolT', [128, 4, TT], BF16)
    ptok = sb('ptok', [128, 512])
    EA = [sb('EA%d' % i, [128, 512], BF16) for i in range(4)]
    EBm = [[sb('EBm%d_%d' % (n, i), [128, 512], BF16) for i in range(2)] for n in range(2)]
    VB = sb('VB', [128, 9, 256], BF16)
    rden = [sb('rden%d' % i, [128, 256]) for i in range(2)]
    ga = [sb('ga%d' % i, [128, TT]) for i in range(2)]
    gp = [sb('gp%d' % i, [128, TT]) for i in range(2)]
    mixT = sb('mixT', [128, 8, TT], BF16)
    u_sb = [sb('u_sb%d' % i, [128, TT + 16]) for i in range(4)]
    ptA = [u_sb[0]]
    ptB = [u_sb[1]]
    c_sb = [sb('c_sb%d' % i, [128, TT]) for i in range(4)]
    aT = sb('aT', [128, 22, TT], BF16)
    qb = aT[:, 0:8, :].rearrange('p (s a) c -> p s (a c)', s=4)
    rt1 = [c_sb[0], c_sb[1]]
    rt2 = [c_sb[2], c_sb[3]]
    cs_p = sb('cs_p', [128, NCH, 2])
    cs_s = sb('cs_s', [128, NCH, 2])
    zero2 = sb('zero2', [128, 2])
    stg = ga[0][:, :].rearrange('p (b c) -> p b c', b=4)
    ostg = ptok[:, 0:256].rearrange('p (t c) -> p t c', t=2)
    banks = [nc.alloc_psum_tensor('bank%d' % i, [128, 512], F32) for i in range(8)]
    trb = [banks[6 + i][:, :].bitcast(BF16).rearrange('p (a b) -> p a b', a=8) for i in range(2)]
    R = {}

    def res(name):
        if name not in R:
            R[name] = Res(name)
        return R[name]
    R['stg'] = res('ga0')
    R['ptA0'] = res('u_sb0')
    R['ptB0'] = res('u_sb1')
    R['trb0'] = res('bank6')
    R['trb1'] = res('bank7')
    for _i in range(2):
        R['rt1_%d' % _i] = res('c_sb%d' % _i)
        R['rt2_%d' % _i] = res('c_sb%d' % (2 + _i))
    R['ostg'] = res('ptok')
    rot = {}

    def nxt(name, n):
        i = rot.get(name, 0)
        rot[name] = i + 1
        return i % n

    bank_lru = list(range(8))

    def bank_get(att=False):
        lim = 8 if att else 6
        i = next(b_ for b_ in bank_lru if b_ < lim)
        bank_lru.remove(i)
        bank_lru.append(i)
        return (banks[i], res('bank%d' % i))

    def trb_get():
        i = nxt('trb', 2)
        bank_lru.remove(6 + i)
        bank_lru.append(6 + i)
        return (trb[i], res('trb%d' % i))

    def op(eng, name, reads=(), writes=(), _a=(), _force=(), **kw):

        def fn(e, name=name, _a=_a, kw=kw):
            return getattr(e, name)(*_a, **kw)
        return S.op(eng, fn, reads=reads, writes=writes, force=_force)

    def dma(eng, out_ap, in_ap, reads=(), writes=(), out=False, slow=False):
        if slow:
            return S.op(eng, lambda e: e.dma_start(out=out_ap, in_=in_ap, allow_slow_non_contiguous=True), reads=reads, writes=writes, dma=True, out=out)
        return S.op(eng, lambda e: e.dma_start(out=out_ap, in_=in_ap), reads=reads, writes=writes, dma=True, out=out)


    def vdup_out(ap):
        return ap.rearrange('p (n u d) -> p n u d', n=2, u=2)

    def vdup_in(ap, rows):
        return ap.rearrange('p (n d) -> p n d', n=2).unsqueeze(2).to_broadcast([rows, 2, 2, 64])
    def vb_slot(c):
        return c % 8 if c % 8 != 7 else 7 + (c // 8) % 2

    pe_last = {}

    def mm(out_ap, lhsT, rhs, start, stop, reads, writes, tp=None):
        kb = tp[0] if tp is not None else 0
        K = lhsT.shape[0]
        groups = set(range(kb // 32, (kb + K - 1) // 32 + 1))
        force = []
        for w in writes:
            prev = pe_last.get(w.name)
            if prev is not None and not (prev[1] & groups):
                force.append(prev[0])
        kw = {} if tp is None else {'tile_position': tp}
        o = op('pe', 'matmul', reads, writes, _a=(out_ap,), _force=force, lhsT=lhsT, rhs=rhs, start=start, stop=stop, **kw)
        for w in writes:
            pe_last[w.name] = (o.id, groups)

    def tr(out_ap, in_ap, n, reads, writes):
        op('pe', 'transpose', reads, writes, _a=(out_ap, in_ap, ident[0:n, 0:n]))

    def precast(dst, src, rows, tag):
        nblk = rows // 128
        for k in range(nblk):
            dma('pool', dst[k * 128:(k + 1) * 128, :], src[k * 128:(k + 1) * 128, :], writes=[res('%s_%d' % (tag, k))])
    precast(s_win, w_in, D, 'sc_win')
    op('pool', 'memset', writes=[res('stg')], _a=(stg[:, :, :], 0.0))
    dma('sp', stg[0:8, 0, :], g1.rearrange('(k p) -> k p', p=128), reads=[res('stg')], writes=[res('stgd_0')])
    dma('sp', stg[8:16, 0, :], g2.rearrange('(k p) -> k p', p=128), reads=[res('stg')], writes=[res('stgd_1')])
    dma('sp', stg[16:32, 0, :], b_gate.rearrange('(k p) -> k p', p=128), reads=[res('stg')], writes=[res('stgd_2')])
    dma('sp', stg[32:36, 0, :], pscale.rearrange('(k p) -> k p', p=128), reads=[res('stg')], writes=[res('stgd_3')])
    dma('sp', stg[36:80, 0, :], conv_b.rearrange('(k p) -> k p', p=128), reads=[res('stg')], writes=[res('stgd_4')])
    cw_rows = conv_w.rearrange('j (c p) -> (j c) p', p=128)
    dma('sp', stg[0:128, 1, :], cw_rows[0:128, :], reads=[res('stg')], writes=[res('stgd_5')])
    dma('sp', stg[0:4, 2, :], cw_rows[128:132, :], reads=[res('stg')], writes=[res('stgd_6')])
    dma('sp', stg[32:120, 2, :], sconv.rearrange('t (c p) -> (t c) p', p=128), reads=[res('stg')], writes=[res('stgd_7')])
    for g in range(4):
        dma('sp', stg[g * 32:g * 32 + 15, 3, :], spool[:, g * 128:(g + 1) * 128], reads=[res('stg')], writes=[res('stgd_sp%d' % g)])
    dma('sp', g3b[:, :], g3.partition_broadcast(128), writes=[res('g3b')])
    dma('sp', ropecb[:, :], ropec.partition_broadcast(128), writes=[res('ropecb')])
    dma('sp', sk_f[:, :], sinks.rearrange('(o s) -> o s', o=1), writes=[res('sk_f')])
    op('pool', 'iota', writes=[res('identf')], _a=(identf[:, :],), pattern=[[1, 128]], base=0, channel_multiplier=-1, allow_small_or_imprecise_dtypes=True)
    op('dve', 'tensor_single_scalar', reads=[res('identf')], writes=[res('ident')], out=ident[:, :], in_=identf[:, :], scalar=0.0, op=ALU.is_equal)
    for _i in range(8):
        op('dve', 'memset', writes=[res('bank%d' % _i)], _a=(banks[_i][:, :], 0.0))
    op('dve', 'tensor_single_scalar', reads=[res('identf')], writes=[res('identF')], out=identF[:, :], in_=identf[:, :], scalar=0.0, op=ALU.is_equal)
    for _b in range(4):
        op('pe', 'transpose', [res('stg'), res('identF')] + [res('stgd_%d' % _j) for _j in range(8)] + [res('stgd_sp%d' % _g) for _g in range(4)], [res('bank0')], _a=(banks[0][:, _b * 128:(_b + 1) * 128], stg[:, _b, :], identF[:, :]))
    _T = banks[0]
    op('act', 'copy', reads=[res('bank0')], writes=[res('g1col')], out=g1col[:, :], in_=_T[:, 0:8])
    op('act', 'copy', reads=[res('bank0')], writes=[res('g2col')], out=g2col[:, :], in_=_T[:, 8:16])
    op('act', 'copy', reads=[res('bank0')], writes=[res('bgcol')], out=bgcol[:, :], in_=_T[:, 16:32])
    op('act', 'copy', reads=[res('bank0')], writes=[res('pscol')], out=pscol[:, :], in_=_T[:, 32:36])
    op('act', 'copy', reads=[res('bank0')], writes=[res('cbcol')], out=cbcol[:, :], in_=_T[:, 36:80])
    _cwf = cwcol[:, :, :].rearrange('p j c -> p (j c)')
    op('dve', 'tensor_copy', reads=[res('bank0')], writes=[res('cwcol')], out=_cwf[:, 0:128], in_=_T[:, 128:256])
    op('dve', 'tensor_copy', reads=[res('bank0')], writes=[res('cwcol')], out=_cwf[:, 128:132], in_=_T[:, 256:260])
    op('dve', 'tensor_copy', reads=[res('bank0')], writes=[res('cs_s%d' % c) for c in range(NCH)], out=cs_s[:, :, :], in_=_T[:, 256 + 32:256 + 120].rearrange('p (t c) -> p c t', t=2))
    op('dve', 'tensor_copy', reads=[res('bank0')], writes=[res('pS1_%d' % g) for g in range(4)], out=pS[1][:, :, 0:15], in_=_T[:, 384:512].rearrange('p (g r) -> p g r', r=32)[:, :, 0:15])
    op('pool', 'memset', writes=[res('ones')], _a=(ones[:, :], 1.0))
    op('pool', 'memset', writes=[res('zero2')], _a=(zero2[:, :], 0.0))
    op('pool', 'memset', writes=[res('cs_p%d' % c) for c in range(NCH)], _a=(cs_p[:, :, :], 0.0))
    op('pool', 'memset', writes=[res('pS0_%d' % g) for g in range(4)], _a=(pS[0][:, :, :], 0.0))
    op('pool', 'memset', writes=[res('VmLo')], _a=(VmLo[:, :], 0.0))
    op('pool', 'memset', writes=[res('VmHi')], _a=(VmHi[:, :], 0.0))
    op('act', 'activation', reads=[res('sk_f')], writes=[res('sk_e')], out=sk_e[:, :], in_=sk_f[:, :], func=AF.Exp)
    for n in range(2):
        op('dve', 'tensor_copy', reads=[res('sk_e')], writes=[res('esrow')], out=esrow[0:1, n, :].rearrange('o (g q) -> o g q', g=8), in_=sk_e[0:1, n * 8:(n + 1) * 8].unsqueeze(2).to_broadcast([1, 8, 64]))
    op('pool', 'memset', writes=[res('VBm%d' % r) for r in range(9)] + [res('VB%d' % r) for r in range(9)], _a=(VB[:, :, :], 0.0))
    for n in range(2):
        for r in range(2):
            dma('sp', EBm[n][r][80:81, :], esrow[0:1, n, :], reads=[res('esrow')], writes=[res('EBm%d_%d_sink' % (n, r))])
    op('pool', 'iota', writes=[res('invcnt')], _a=(invcnt[:, 0, :],), pattern=[[1, 16]], base=1, channel_multiplier=0, allow_small_or_imprecise_dtypes=True)
    for g in (1, 2, 3):
        op('dve', 'tensor_scalar', reads=[res('invcnt')], writes=[res('invcnt')], out=invcnt[:, g, :], in0=invcnt[:, 0, :], scalar1=float(POOLW[g]), scalar2=None, op0=ALU.min)
    op('dve', 'tensor_scalar', reads=[res('invcnt')], writes=[res('invcnt')], out=invcnt[:, 0, :], in0=invcnt[:, 0, :], scalar1=float(POOLW[0]), scalar2=None, op0=ALU.min)
    op('dve', 'reciprocal', reads=[res('invcnt')], writes=[res('invcnt')], out=invcnt[:, :, :], in_=invcnt[:, :, :])

    def load_cast(dst_bf, dst_rows, src, nrows, rname, transpose_to=None):
        r0 = dst_rows
        dma('sp', cst_f[r0:r0 + nrows, :], src, writes=[res('cst_f')])
        if transpose_to is None:
            op('dve', 'tensor_copy', reads=[res('cst_f')], writes=[res(rname)], out=vdup_out(dst_bf[r0:r0 + nrows, :]), in_=vdup_in(cst_f[r0:r0 + nrows, :], nrows))
        else:
            op('dve', 'tensor_copy', reads=[res('cst_f')], writes=[res('cst_b')], out=cst_b[r0:r0 + nrows, :], in_=cst_f[r0:r0 + nrows, :])
            tb, tr = trb_get()
            op('pe', 'transpose', reads=[res('cst_b'), res('ident')], writes=[tr], _a=(tb[:, 0, 0:nrows], cst_b[0:nrows, :], ident[0:nrows, 0:nrows]))
            op('act', 'copy', reads=[tr], writes=[res(rname)], out=transpose_to[:, 0:nrows], in_=tb[:, 0, 0:nrows])
    load_cast(Vcs, 0, cswav, 128, 'Vcs')
    load_cast(Vcm, 64, cmetav, 16, 'Vcm')
    load_cast(None, 0, cswak, 128, 'KTcs', transpose_to=KTcs)
    load_cast(None, 0, cmetak, 16, 'KTcm', transpose_to=KTcm)
    precast(s_grp, w_grp, 512, 'sc_grp')
    precast(s_ao, w_ao, D, 'sc_ao')
    precast(s_po, w_po, 512, 'sc_po')
    precast(s_out, w_out, D, 'sc_out')

    def kp(ap):
        return ap.rearrange('(k p) c -> p k c', p=128)

    def tile_units():
        u = []
        for c0, ncol in ((1024, 256), (0, 512), (512, 512)):
            for kh in range(2):
                u.append(('qkv', kp(s_win)[:, kh * 4:kh * 4 + 4, c0:c0 + ncol], (4, ncol), ['sc_win_%d' % k for k in range(kh * 4, kh * 4 + 4)]))
        for half in range(2):
            u.append(('p', kp(s_win)[:, :, 1280 + half * 256:1280 + half * 256 + 256], (8, 256), ['sc_win_%d' % k for k in range(8)]))
        u.append(('grp', s_grp.rearrange('(g c) d -> c g d', c=128), (4, 128), ['sc_grp_%d' % k for k in range(4)]))
        for jp in range(4):
            u.append(('ao', kp(s_ao)[:, :, jp * 256:(jp + 1) * 256], (8, 256), ['sc_ao_%d' % k for k in range(8)]))
            u.append(('po', kp(s_po)[:, :, jp * 256:(jp + 1) * 256], (4, 256), ['sc_po_%d' % k for k in range(4)]))
            u.append(('ga', kp(s_win)[:, :, 1792 + jp * 256:1792 + jp * 256 + 256], (8, 256), ['sc_win_%d' % k for k in range(8)]))
            u.append(('gp', kp(s_win)[:, :, 2816 + jp * 256:2816 + jp * 256 + 256], (8, 256), ['sc_win_%d' % k for k in range(8)]))
        for half in range(2):
            for kh in range(2):
                u.append(('out', kp(s_out)[:, kh * 4:kh * 4 + 4, half * 512:half * 512 + 512], (4, 512), ['sc_out_%d' % k for k in range(kh * 4, kh * 4 + 4)]))
        for ip in range(11):
            u.append(('upg', kp(s_up)[:, :, ip * 256:(ip + 1) * 256], (8, 256), ['sc_up_%d' % k for k in range(8)]))
            u.append(('upv', kp(s_up)[:, :, DFF + ip * 256:DFF + (ip + 1) * 256], (8, 256), ['sc_up_%d' % k for k in range(8)]))
        for half in range(2):
            for kq in range(6):
                k0, k1 = (kq * 4, min(kq * 4 + 4, 22))
                u.append(('down', kp(s_down)[:, k0:k1, half * 512:half * 512 + 512], (k1 - k0, 512), ['sc_down_%d' % k for k in range(k0, k1)]))
        return u
    _tu = tile_units()
    _nmix = sum(1 for u_ in _tu if u_[0] not in ('upg', 'upv', 'down'))
    mix_units, ffn_units = _tu[:_nmix], _tu[_nmix:]
    assert all(u_[0] in ('upg', 'upv', 'down') for u_ in ffn_units)
    if NTILE >= 1:
        all_units = mix_units + mix_units + ffn_units + ffn_units
        for _t in range(1, NTILE):
            all_units = all_units + _tu
    else:
        all_units = list(_tu)
    ws = {'loaded': 0, 'used': 0}

    def w_issue():
        i = ws['loaded']
        if i >= len(all_units):
            return
        kind, src, (a, b), deps = all_units[i]
        sl = i % NSLOT
        dst = wslot[sl][:, 0:a * b].rearrange('p (k c) -> p k c', k=a)
        dma('sp', dst, src, reads=[res(d) for d in deps], writes=[res('wslot%d' % sl)])
        ws['loaded'] += 1

    def w_get(kind):
        i = ws['used']
        k, src, (a, b), deps = all_units[i]
        assert k == kind, (k, kind)
        assert i < ws['loaded']
        sl = i % NSLOT
        ws['used'] += 1
        return (wslot[sl][:, 0:a * b].rearrange('p (k c) -> p k c', k=a), res('wslot%d' % sl))
    for _ in range(NSLOT):
        w_issue()
    xrot = {'i': 0}

    def load_x_sub(tile, s):
        i = xrot['i'] % NX
        xrot['i'] += 1
        if tile < 0:
            dma('sp', x_sb[i][0:16, :], meta, writes=[res('x_sb%d' % i)])
            dma('sp', x_sb[i][16:32, :], xs, writes=[res('x_sb%d' % i)])
            return (x_sb[i], res('x_sb%d' % i), 32)
        r0 = tile * TT + s * 128
        dma('sp', x_sb[i][:, :], xp[r0:r0 + 128, :], writes=[res('x_sb%d' % i)])
        return (x_sb[i], res('x_sb%d' % i), 128)

    def rms_rstd(xt, xr, npr, junk=None):
        i = nxt('st_ss', 2)
        stt, sr = (st_ss[i], res('st_ss%d' % i))
        if junk is None:
            j = nxt('hb', 2)
            hbt, hr = (hb[j], res('hb%d' % j))
            jw = [hr]
            jt = hbt
        else:
            hbt, hr = (None, None)
            jt, jw = junk
        op('act', 'activation', reads=[xr], writes=jw + [sr], out=jt[0:npr, :], in_=xt[0:npr, :], func=AF.Square, accum_out=stt[0:npr, 0:1])
        op('dve', 'tensor_scalar', reads=[sr], writes=[sr], out=stt[0:npr, 1:2], in0=stt[0:npr, 0:1], scalar1=1.0 / D, scalar2=EPS, op0=ALU.mult, op1=ALU.add)
        op('act', 'activation', reads=[sr], writes=[sr], out=stt[0:npr, 2:3], in_=stt[0:npr, 1:2], func=AF.Sqrt)
        op('dve', 'reciprocal', reads=[sr], writes=[sr], out=stt[0:npr, 3:4], in_=stt[0:npr, 2:3])
        return (stt[0:npr, 3:4], sr, hbt, hr)

    def norm_pre(xt, xr, npr):
        rstd, sr, hbt, hr = rms_rstd(xt, xr, npr)
        op('dve', 'tensor_scalar', reads=[xr, sr], writes=[hr], out=hbt[0:npr, :], in0=xt[0:npr, :], scalar1=rstd, scalar2=None, op0=ALU.mult)
        return (hbt, hr)

    def norm_post(s, npr, hbt, hr, gcol, gres, dst, dname):
        tb, tr = trb_get()
        for j in range(8):
            op('pe', 'transpose', reads=[hr, res('ident')], writes=[tr], _a=(tb[:, j, 0:npr], hbt[0:npr, j * 128:(j + 1) * 128], ident[0:npr, 0:npr]))
        op('dve', 'tensor_tensor', reads=[tr, gres], writes=[res('%s%d' % (dname, s))], out=dst[:, :, s * 128:s * 128 + npr], in0=tb[:, :, 0:npr], in1=gcol[:, :].unsqueeze(2).to_broadcast([128, 8, npr]), op=ALU.mult)

    def norm_T(xl, gcol, gres, dst=None, dname='hT'):
        dst = hT if dst is None else dst
        for s, (xt, xr, npr) in enumerate(xl):
            hbt, hr = norm_pre(xt, xr, npr)
            norm_post(s, npr, hbt, hr, gcol, gres, dst, dname)

    def rope_tables(tile):
        nsub = 1 if tile < 0 else 4
        if tile < 0:
            op('pool', 'iota', writes=[res('posf')], _a=(posf[:, 0:1],), pattern=[[1, 1]], base=0, channel_multiplier=1, allow_small_or_imprecise_dtypes=True)
            op('dve', 'tensor_single_scalar', reads=[res('posf')], writes=[res('posf')], out=posf[:, 1:2], in_=posf[:, 0:1], scalar=16.0, op=ALU.is_ge)
            op('dve', 'scalar_tensor_tensor', reads=[res('posf')], writes=[res('posf')], out=posf[:, 0:1], in0=posf[:, 1:2], scalar=1024.0, in1=posf[:, 0:1], op0=ALU.mult, op1=ALU.add)
        else:
            op('pool', 'iota', writes=[res('posf')], _a=(posf[:, 0:4],), pattern=[[128, 4]], base=NMETA + tile * TT, channel_multiplier=1, allow_small_or_imprecise_dtypes=True)
        R_ = [res('posf'), res('ropecb'), res('rp_u'), res('rp_r'), res('rp_i'), res('rp_f')]
        for s in range(nsub):
            op('dve', 'tensor_scalar', reads=R_, writes=[res('rp_u')], out=rp_u[:, s, :], in0=ropecb[:, 0:32], scalar1=posf[:, s:s + 1], scalar2=None, op0=ALU.mult)
        sl = slice(0, nsub)
        C1 = 6.28125
        C2 = float(np.float32(2.0 * math.pi - C1))
        C3 = float(np.float32(2.0 * math.pi - C1 - C2))
        I2P = float(np.float32(1.0 / (2.0 * math.pi)))

        def reduce_(t, tres, consts):
            op('dve', 'tensor_scalar', reads=R_, writes=[res('rp_f')], out=rp_f[:, sl, :], in0=t[:, sl, :], scalar1=I2P, scalar2=None, op0=ALU.mult)
            op('dve', 'tensor_copy', reads=R_, writes=[res('rp_i')], out=rp_i[:, sl, :], in_=rp_f[:, sl, :])
            op('dve', 'tensor_copy', reads=R_, writes=[res('rp_f')], out=rp_f[:, sl, :], in_=rp_i[:, sl, :])
            for c_ in consts:
                op('dve', 'scalar_tensor_tensor', reads=R_, writes=[tres], out=t[:, sl, :], in0=rp_f[:, sl, :], scalar=-c_, in1=t[:, sl, :], op0=ALU.mult, op1=ALU.add)
        reduce_(rp_u, res('rp_u'), (C1, C2, C3))
        reduce_(rp_u, res('rp_u'), (C1, C2))
        op('act', 'activation', reads=[res('rp_u')], writes=[res('sin_t')], out=sin_t[:, sl, :], in_=rp_u[:, sl, :], func=AF.Sin, scale=0.9999998)
        op('dve', 'tensor_scalar', reads=R_, writes=[res('rp_r')], out=rp_r[:, sl, :], in0=rp_u[:, sl, :], scalar1=float(np.float32(math.pi / 2.0)), scalar2=None, op0=ALU.add)
        reduce_(rp_r, res('rp_r'), (C1, C2))
        op('act', 'activation', reads=[res('rp_r')], writes=[res('cos_t')], out=cos_t[:, sl, :], in_=rp_r[:, sl, :], func=AF.Sin, scale=0.9999998)
        op('dve', 'tensor_scalar', reads=[res('sin_t')], writes=[res('nsin_t')], out=nsin_t[:, sl, :], in0=sin_t[:, sl, :], scalar1=-1.0, scalar2=None, op0=ALU.mult)

    def bc(tab, s, npr, nh):
        return tab[0:npr, s, :].unsqueeze(1).to_broadcast([npr, nh, 32])

    def rope_bank(bk, br, s, npr, nh, col0, out_ap, out_res, out_eng):
        i = nxt('rt', 2)
        t1, t2 = (rt1[i], rt2[i])
        r1, r2 = (res('rt1_%d' % i), res('rt2_%d' % i))
        v4 = bk[0:npr, col0:col0 + nh * 64].rearrange('p (h two d) -> p h two d', two=2, d=32)
        t1v = t1[0:npr, 0:nh * 64].rearrange('p (h two d) -> p h two d', two=2, d=32)
        t2v = t2[0:npr, 0:nh * 64].rearrange('p (h two d) -> p h two d', two=2, d=32)
        tabs = [res('cos_t'), res('sin_t'), res('nsin_t')]
        op('dve', 'tensor_tensor', reads=[br] + tabs, writes=[r1], out=t1v, in0=v4, in1=cos_t[0:npr, s, :].unsqueeze(1).unsqueeze(1).to_broadcast([npr, nh, 2, 32]), op=ALU.mult)
        op('dve', 'tensor_tensor', reads=[br] + tabs, writes=[r2], out=t2v[:, :, 0, :], in0=v4[:, :, 1, :], in1=bc(nsin_t, s, npr, nh), op=ALU.mult)
        op('dve', 'tensor_tensor', reads=[br] + tabs, writes=[r2], out=t2v[:, :, 1, :], in0=v4[:, :, 0, :], in1=bc(sin_t, s, npr, nh), op=ALU.mult)
        op(out_eng, 'tensor_tensor', reads=[r1, r2], writes=(out_res if isinstance(out_res, list) else [out_res]), out=out_ap, in0=t1[0:npr, 0:nh * 64].rearrange('p (h e) -> p h e', e=64), in1=t2[0:npr, 0:nh * 64].rearrange('p (h e) -> p h e', e=64), op=ALU.add)

    def attn_scores(n_q, q0, qres, key_banks):
        ncol = 8 * n_q
        kbs = []
        for kb in key_banks:
            if isinstance(kb, tuple) and kb[0] == 'M':
                kbs.append((kb[1], (kb[2], kb[3])))
            elif isinstance(kb, tuple):
                kbs.append((kb[1], 'E'))
            else:
                kbs.append((kb, None))
        bks = [[bank_get(att=True) for _ in kbs] for n in range(2)]
        for bi, (segs, mg) in enumerate(kbs):
            for kt_ap, kres, v_ap, vres, rb, nk in segs:
                for n in range(2):
                    bk, br = bks[n][bi]
                    mm(bk[rb:rb + nk, 0:ncol].rearrange('p (g q) -> p g q', g=8), kt_ap[n * 64:(n + 1) * 64, :], QO[n * 64:(n + 1) * 64, :, q0:q0 + n_q], True, True, [kres, qres], [br], tp=(n * 64, rb))
        eb_all = []
        for n in range(2):
            ebufs = []
            for bi, (segs, mg) in enumerate(kbs):
                bk, br = bks[n][bi]
                lo = min((sg[4] for sg in segs))
                hi = max((sg[4] + sg[5] for sg in segs))
                if mg is None:
                    ei = nxt('EA', 4)
                    et, er = EA[ei], res('EA_%d' % ei)
                    rds, pvsegs = [br], segs
                elif mg == 'E':
                    ei = nxt('EBm%d' % n, 2)
                    et, er = EBm[n][ei], res('EBm%d_%d' % (n, ei))
                    rds, pvsegs = [br], segs
                else:
                    ei = nxt('EBm%d' % n, 2)
                    et, er = EBm[n][ei], res('EBm%d_%d' % (n, ei))
                    rds = [br, res('EBm%d_%d_sink' % (n, ei))]
                    pvsegs = [(None, None, mg[0], mg[1], 0, 81)]
                op('act', 'activation', reads=rds, writes=[er], out=et[lo:hi, 0:ncol], in_=bk[lo:hi, 0:ncol], func=AF.Exp, scale=0.125)
                ebufs.append((et, er, pvsegs, mg is not None and mg != 'E'))
            eb_all.append(ebufs)
        return eb_all

    def attn_pv(n_q, q0, eb_all, ores):
        ncol = 8 * n_q
        for n in range(2):
            ebufs = eb_all[n]
            pn, prn = bank_get(att=True)
            pd, prd = bank_get(att=True)
            has_merged = any(e[3] for e in ebufs)
            flat = [(et, er, sg) for et, er, segs, mgd in ebufs for sg in segs]
            grp = lambda sg: ('A' if sg[5] > 64 else ('hi' if sg[4] >= 64 else 'lo'))
            his = [f for f in flat if grp(f[2]) == 'hi']
            As = [f for f in flat if grp(f[2]) == 'A']
            los = [f for f in flat if grp(f[2]) == 'lo']
            num_order = his + As + los
            den_order = los + ([] if has_merged else ['sink']) + As + his
            k = 0
            for et, er, (kt_ap, kres, v_ap, vres, rb, nk) in num_order:
                mm(pn[:, 0:ncol].rearrange('p (g q) -> p g q', g=8), v_ap[rb:rb + nk, n * 128:(n + 1) * 128], et[rb:rb + nk, 0:ncol].rearrange('p (g q) -> p g q', g=8), k == 0, k == len(num_order) - 1, (vres if isinstance(vres, list) else [vres]) + [er], [prn], tp=(rb, 0))
                k += 1
            k = 0
            for item in den_order:
                if item == 'sink':
                    mm(pd[:, 0:ncol].rearrange('p (g q) -> p g q', g=8), ones[0:1, 0:128], esrow[0:1, n, :].rearrange('o (g q) -> o g q', g=8)[:, :, 0:n_q], k == 0, k == len(den_order) - 1, [res('ones'), res('esrow')], [prd], tp=(0, 0))
                else:
                    et, er, (kt_ap, kres, v_ap, vres, rb, nk) = item
                    mm(pd[:, 0:ncol].rearrange('p (g q) -> p g q', g=8), ones[rb:rb + nk, 0:128], et[rb:rb + nk, 0:ncol].rearrange('p (g q) -> p g q', g=8), k == 0, k == len(den_order) - 1, [res('ones'), er], [prd], tp=(rb, 0))
                k += 1
            gi = nxt('ga', 2)
            nsb, nsr = ga[gi], res('ga%d' % gi)
            rd, rr = gp[gi], res('gp%d' % gi)
            op('act', 'activation', reads=[prd], writes=[rr], out=rd[:, 0:ncol], in_=pd[:, 0:ncol], func=AF.Ln)
            op('act', 'activation', reads=[rr], writes=[rr], out=rd[:, 0:ncol], in_=rd[:, 0:ncol], func=AF.Exp, scale=-1.0)
            op('dve', 'tensor_copy', reads=[prn], writes=[nsr], out=nsb[:, 0:ncol], in_=pn[:, 0:ncol])
            for par in range(2):
                ps_ = slice(par * 64, (par + 1) * 64)
                nv = nsb[ps_, 0:ncol].rearrange('p (j two q) -> p j two q', two=2, q=n_q)[:, :, par, :]
                rv = rd[ps_, 0:ncol].rearrange('p (j two q) -> p j two q', two=2, q=n_q)[:, :, par, :]
                op('pool', 'tensor_tensor', reads=[nsr, rr], writes=[ores], out=QO[ps_, n * 4:(n + 1) * 4, q0:q0 + n_q], in0=nv, in1=rv, op=ALU.mult)

    def attention(n_q, q0, qres, key_banks, ores):
        attn_pv(n_q, q0, attn_scores(n_q, q0, qres, key_banks), ores)

    def pool_windows(P, Pres, L, g):
        i = nxt('ptAB', 1)
        tA, tB = (ptA[i], ptB[i])
        rA, rB = (res('ptA%d' % i), res('ptB%d' % i))
        op('pool', 'tensor_tensor', reads=[Pres], writes=[rA], out=tA[:, 1:L], in0=P[:, 1:L], in1=P[:, 0:L - 1], op=ALU.add)
        if g == 0:
            return (tA, rA)
        op('pool', 'tensor_tensor', reads=[rA], writes=[rB], out=tB[:, 3:L], in0=tA[:, 3:L], in1=tA[:, 1:L - 2], op=ALU.add)
        if g == 1:
            return (tB, rB)
        op('pool', 'tensor_tensor', reads=[rB], writes=[rA], out=tA[:, 7:L], in0=tB[:, 7:L], in1=tB[:, 3:L - 4], op=ALU.add)
        if g == 2:
            return (tA, rA)
        op('pool', 'tensor_tensor', reads=[rA], writes=[rB], out=tB[:, 15:L], in0=tA[:, 15:L], in1=tA[:, 7:L - 8], op=ALU.add)
        return (tB, rB)

    def run_tile(tile, xl, next_x_cb, hoist_cb, pre_normed, split=False):
        special = tile < 0
        NT = 32 if special else TT
        nsub = len(xl)
        hres = [res('hT%d' % s) for s in range(nsub)]
        h2x, h2n = (h2Ts, 'h2Ts') if special else (h2T, 'h2T')
        if not pre_normed:
            norm_T(xl, g1col, res('g1col'))
            rope_tables(tile)
        if _DBG.get('stage') == 'A':
            return
        thirds = ((0, 512), (512, 512), (1024, 256))
        kt_jobs = []
        for ti in (2, 0, 1):
            c0, ncol = thirds[ti]
            bks = [bank_get(att=(ti < 2)) for _ in range(nsub)]
            if ti == 2 and not special:
                bkx, brx = bank_get()
            for kh in range(2):
                wv, wr = w_get('qkv')
                for kk in range(4):
                    k = kh * 4 + kk
                    for s, (xt, xr, npr) in enumerate(xl):
                        mm(bks[s][0][0:npr, 0:ncol], hT[:, k, s * 128:s * 128 + npr], wv[:, kk, :], k == 0, k == 7, [hres[s], wr], [bks[s][1]])
                if ti != 2:
                    w_issue()
            if ti == 2 and not special:
                i1 = ws['used'] - 2
                for s, (xt, xr, npr) in enumerate(xl):
                    for kh in range(2):
                        sl = (i1 + kh) % NSLOT
                        wv2 = wslot[sl][:, 0:4 * 256].rearrange('p (k c) -> p k c', k=4)
                        for kk in range(4):
                            k = kh * 4 + kk
                            mm(bkx[0:64, s * 128:(s + 1) * 128], hT[:, k, s * 128 + 64:s * 128 + 128], wv2[:, kk, 128:256], k == 0, k == 7, [hres[s], res('wslot%d' % sl)], [brx])
                w_issue()
                w_issue()
            if ti == 2 and special:
                bkm, brm = bank_get()
                i1 = ws['used'] - 2
                for kh in range(2):
                    sl = (i1 + kh) % NSLOT
                    wv2 = wslot[sl][:, 0:4 * 256].rearrange('p (k c) -> p k c', k=4)
                    for kk in range(4):
                        k = kh * 4 + kk
                        mm(bkm[64:80, 0:128], hT[:, k, 0:16], wv2[:, kk, 128:256], k == 0, k == 7, [hres[0], res('wslot%d' % sl)], [brm], tp=(0, 64))
                        mm(bkm[0:16, 0:128], hT[:, k, 16:32], wv2[:, kk, 128:256], k == 0, k == 7, [hres[0], res('wslot%d' % sl)], [brm])
                op('act', 'copy', reads=[brm], writes=[res('VmHi')], out=vdup_out(VmHi[64:80, :]), in_=vdup_in(bkm[64:80, 0:128], 16))
                for r in range(9):
                    op('pool', 'tensor_copy', reads=[res('VmHi')], writes=[res('VBm%d' % r)], out=VB[64:80, r, :], in_=VmHi[64:80, :])
                op('act', 'copy', reads=[brm], writes=[res('Vs')], out=vdup_out(Vs[0:16, :]), in_=vdup_in(bkm[0:16, 0:128], 16))
                w_issue()
                w_issue()
            for s, (xt, xr, npr) in enumerate(xl):
                bk, br = bks[s]
                if ti < 2:
                    n = ti
                    outv = qb[0:npr, s, :].rearrange('p (g n d) -> p n g d', n=2, d=64)[:, n]
                    rope_bank(bk, br, s, npr, 8, 0, outv, [res('aT%d' % (2 * s)), res('aT%d' % (2 * s + 1))], 'pool')
                else:
                    i = nxt('kf', 2)
                    kft, kbtt, vft = (kf[i], kbt[s], vf[i])
                    kfr, kbr, vfr = (res('kf%d' % i), res('kbt%d' % s), res('vf%d' % i))
                    rope_bank(bk, br, s, npr, 2, 0, kft[0:npr, :].rearrange('p (h e) -> p h e', e=64), kfr, 'pool')
                    op('act', 'copy', reads=[kfr], writes=[kbr], out=kbtt[0:npr, :], in_=kft[0:npr, :])
                    if special:
                        op('act', 'copy', reads=[br], writes=[res('VmLo')], out=vdup_out(VmLo[0:16, :]), in_=vdup_in(bk[0:16, 128:256], 16))
                        op('act', 'copy', reads=[br], writes=[vfr], out=vft[0:32, :], in_=bk[0:32, 128:256])
                        dma('sp', o_metak, kft[0:16, :], reads=[kfr], out=True)
                        dma('sp', o_sk, kft[16:32, :], reads=[kfr], out=True)
                        dma('sp', o_metav, vft[0:16, :], reads=[vfr], out=True)
                        dma('sp', o_sv, vft[16:32, :], reads=[vfr], out=True)
                    else:
                        gsub = tile * 4 + s
                        vslot = gsub % 8
                        op('act', 'copy', reads=[br], writes=[res('VR%d' % vslot)], out=vdup_out(VR[:, vslot, :]), in_=vdup_in(bk[:, 128:256], 128))
                        se, so = vb_slot(tile * 8 + 2 * s), vb_slot(tile * 8 + 2 * s + 1)
                        op('act', 'copy', reads=[br], writes=[res('VB%d' % se)], out=vdup_out(VB[0:64, se, :]), in_=vdup_in(bk[0:64, 128:256], 64))
                        op('act', 'copy', reads=[brx], writes=[res('VB%d' % so)], out=vdup_out(VB[0:64, so, :]), in_=vdup_in(bkx[0:64, s * 128:(s + 1) * 128], 64))
                        if tile == NTILE - 1 and s == 3:
                            op('act', 'copy', reads=[br], writes=[vfr], out=vft[:, :], in_=bk[:, 128:256])
                            dma('sp', o_swak, kft[:, :], reads=[kfr], out=True)
                            dma('sp', o_swav, vft[:, :], reads=[vfr], out=True)
                    kt_jobs.append((s, npr, kbtt, kbr))
        yield 'projB_done'
        if special:
            precast(s_up, w_up, D, 'sc_up')
            precast(s_down, w_down, DFF, 'sc_down')
        pun = [w_get('p'), w_get('p')]
        pbanks = []
        for g in range(4):
            bk, br = bank_get()
            wv, wr = pun[g // 2]
            for k in range(8):
                mm(bk[:, 0:NT], wv[:, k, g % 2 * 128:g % 2 * 128 + 128], hT[:, k, 0:NT], k == 0, k == 7, hres + [wr], [br])
            pbanks.append((bk, br))
        if special or tile == NTILE - 1:
            s_last = nsub - 1
            npr = xl[s_last][2]
            bk, br = bank_get()
            for half in range(2):
                wv, wr = pun[half]
                for k in range(8):
                    mm(bk[0:npr, half * 256:(half + 1) * 256], hT[:, k, s_last * 128:s_last * 128 + npr], wv[:, k, :], k == 0, k == 7, [hres[s_last], wr], [br])
            op('act', 'copy', reads=[br], writes=[res('ptok')], out=ptok[0:npr, :], in_=bk[0:npr, :])
            if special:
                dma('sp', o_spool, ptok[17:32, :], reads=[res('ptok')], out=True)
            else:
                dma('sp', o_ppool, ptok[113:128, :], reads=[res('ptok')], out=True)
        w_issue()
        w_issue()
        for g in range(4):
            bk, br = pbanks[g]
            w = POOLW[g]
            if special:
                op('act', 'copy', reads=[br], writes=[res('pS0_%d' % g)], out=pS[0][:, g, 15:31], in_=bk[:, 0:16])
                op('act', 'copy', reads=[br], writes=[res('pS1_%d' % g)], out=pS[1][:, g, 15:31], in_=bk[:, 16:32])
                fin, fr = pool_windows(pS[0][:, g, :], res('pS0_%d' % g), 31, g)
                op('dve', 'tensor_tensor', reads=[fr, res('invcnt')], writes=[res('ptmp')], out=ptmp[:, 0:16], in0=fin[:, 15:31], in1=invcnt[:, g, :], op=ALU.mult)
                op('dve', 'tensor_tensor', reads=[res('ptmp'), res('pS0_%d' % g)], writes=[res('pmT%d' % g)], out=pmT[:, g, 0:16], in0=ptmp[:, 0:16], in1=pS[0][:, g, 15:31], op=ALU.subtract)
                fin, fr = pool_windows(pS[1][:, g, :], res('pS1_%d' % g), 31, g)
                op('dve', 'scalar_tensor_tensor', reads=[fr, res('pS1_%d' % g)], writes=[res('pmT%d' % g)], out=pmT[:, g, 16:32], in0=fin[:, 15:31], scalar=1.0 / w, in1=pS[1][:, g, 15:31], op0=ALU.mult, op1=ALU.subtract)
            else:
                op('act', 'copy', reads=[br], writes=[res('pTc%d' % g)], out=pT[:, g, 15:15 + TT], in_=bk[:, 0:TT])
                fin, fr = pool_windows(pT[:, g, :], res('pTc%d' % g), 15 + TT, g)
                op('dve', 'scalar_tensor_tensor', reads=[fr, res('pTc%d' % g)], writes=[res('pmT%d' % g)], out=pmT[:, g, 0:TT], in0=fin[:, 15:15 + TT], scalar=1.0 / w, in1=pT[:, g, 15:15 + TT], op0=ALU.mult, op1=ALU.subtract)
        if special:
            op('pool', 'tensor_copy', reads=[res('pS0_%d' % g) for g in range(4)], writes=[res('pTc%d' % g) for g in range(4)], out=pT[:, :, 0:15], in_=pS[0][:, :, 16:31])
        elif tile < NTILE - 1:
            op('pool', 'tensor_copy', reads=[res('pTc%d' % g) for g in range(4)], writes=[res('pTc%d' % g) for g in range(4)], out=pT[:, :, 0:15], in_=pT[:, :, TT:TT + 15])
        for (s, npr, kbtt, kbr) in kt_jobs:
            tb, tr = trb_get()
            op('pe', 'transpose', reads=[kbr, res('ident')], writes=[tr], _a=(tb[:, 0, 0:npr], kbtt[0:npr, :], ident[0:npr, 0:npr]))
            if special:
                op('act', 'copy', reads=[tr], writes=[res('KTmeta')], out=KT[:, 0:16], in_=tb[:, 0, 0:16])
                op('act', 'copy', reads=[tr], writes=[res('KTs')], out=KTs[:, 0:16], in_=tb[:, 0, 16:32])
            else:
                gt = (tile * TT + s * 128) % 1024
                op('act', 'copy', reads=[tr], writes=[res('KT%d' % (gt // 128))], out=KT[:, 16 + gt:16 + gt + 128], in_=tb[:, 0, :])
        qchunks = [res('QO%d' % c) for c in range(max(1, NT // 64))]
        for s, (xt, xr, npr) in enumerate(xl):
            tb, tr = trb_get()
            for g in range(8):
                op('pe', 'transpose', reads=[res('aT%d' % (2 * s)), res('aT%d' % (2 * s + 1)), res('ident')], writes=[tr], _a=(tb[:, g, 0:npr], qb[0:npr, s, g * 128:(g + 1) * 128], ident[0:npr, 0:npr]))
            wr_ = [qchunks[0]] if special else [qchunks[2 * s], qchunks[2 * s + 1]]
            op('dve', 'tensor_copy', reads=[tr], writes=wr_, out=QO[:, :, s * 128:s * 128 + npr], in_=tb[:, :, 0:npr])
        if special:
            attention(16, 0, qchunks[0], [[(KT[:, 0:16], res('KTmeta'), VmLo, res('VmLo'), 0, 16)]], qchunks[0])
            if _DBG.get('stage') == 'D1':
                return
            _kbA = [(KTcs[:, :], res('KTcs'), Vcs, res('Vcs'), 0, 128)]
            _kbB1 = (KTs[:, 0:16], res('KTs'), Vs, res('Vs'), 0, 16)
            _kbB2 = (KTcm[:, 0:16], res('KTcm'), Vcm, res('Vcm'), 64, 16)
            _st = _DBG.get('stage')
            if _st == 'D2a':
                attention(16, 16, qchunks[0], [_kbA], qchunks[0]); return
            if _st == 'D2b':
                attention(16, 16, qchunks[0], [[_kbB1]], qchunks[0]); return
            if _st == 'D2c':
                attention(16, 16, qchunks[0], [[_kbB1, _kbB2]], qchunks[0]); return
            if _st == 'D2d':
                attention(16, 16, qchunks[0], [[_kbB2]], qchunks[0]); return
            attention(16, 16, qchunks[0], [_kbA, [_kbB1, _kbB2]], qchunks[0])
        else:
            att_pending = None
            for cl in range(8):
                c = tile * 8 + cl

                def ktc(j):
                    col = 16 + j * 64 % 1024
                    return (col, res('KT%d' % (j * 64 % 1024 // 128)))

                def vsl(j):
                    sl = j // 2 % 8
                    return (VR[:, sl, :], res('VR%d' % sl))
                kb_list = []

                def merged(j):
                    col, kr = ktc(j)
                    return ('M', [(KT[:, col:col + 64], kr, None, None, 0, 64), (KT[:, 0:16], res('KTmeta'), None, None, 64, 16)],
                            VB[:, vb_slot(j), :], [res('VB%d' % vb_slot(j)), res('VBm%d' % vb_slot(j))])
                if c % 2 == 0:
                    if c >= 2:
                        col, kr = ktc(c - 2)
                        va, vr = vsl(c - 2)
                        kb_list.append([(KT[:, col:col + 128], kr, va, vr, 0, 128)])
                    kb_list.append(merged(c))
                else:
                    col, kr = ktc(c - 1)
                    va, vr = vsl(c - 1)
                    kb_list.append([(KT[:, col:col + 128], kr, va, vr, 0, 128)])
                    if c >= 3:
                        kb_list.append(merged(c - 2))
                    else:
                        kb_list.append(('E', [(KT[:, 0:16], res('KTmeta'), VmHi, res('VmHi'), 64, 16)]))
                sc = attn_scores(64, cl * 64, qchunks[cl], kb_list)
                if att_pending is not None:
                    attn_pv(*att_pending)
                att_pending = (64, cl * 64, sc, qchunks[cl])
            attn_pv(*att_pending)
        wv, wr = w_get('grp')
        for g in range(4):
            bk, br = bank_get()
            mm(bk[:, 0:NT], wv[:, g, :], pmT[:, g, 0:NT], True, True, [res('pmT%d' % g), wr], [br])
            op('act', 'activation', reads=[br, res('pscol')], writes=[res('poolT%d' % g)], out=poolT[:, g, 0:NT], in_=bk[:, 0:NT], func=AF.Copy, scale=pscol[:, g:g + 1])
        w_issue()
        for jp in range(4):
            wao, rao = w_get('ao')
            wpo, rpo = w_get('po')
            wga, rga = w_get('ga')
            wgp, rgp = w_get('gp')
            for jj in range(2):
                jc = jp * 2 + jj
                cs = slice(jj * 128, jj * 128 + 128)
                bA, rA = bank_get()
                for k in range(8):
                    mm(bA[:, 0:NT], wao[:, k, cs], QO[:, k, 0:NT], k == 0, k == 7, qchunks + [rao], [rA])
                bP, rP = bank_get()
                for k in range(4):
                    mm(bP[:, 0:NT], wpo[:, k, cs], poolT[:, k, 0:NT], k == 0, k == 3, [res('poolT%d' % k), rpo], [rP])
                bGa, rGa = bank_get()
                for k in range(8):
                    mm(bGa[:, 0:NT], wga[:, k, cs], hT[:, k, 0:NT], k == 0, k == 7, hres + [rga], [rGa])
                bGp, rGp = bank_get()
                for k in range(8):
                    mm(bGp[:, 0:NT], wgp[:, k, cs], hT[:, k, 0:NT], k == 0, k == 7, hres + [rgp], [rGp])
                gi = nxt('ga', 2)
                gat, gpt = (ga[gi], gp[gi])
                gar, gpr = (res('ga%d' % gi), res('gp%d' % gi))
                op('act', 'activation', reads=[rGa, res('bgcol')], writes=[gar], out=gat[:, 0:NT], in_=bGa[:, 0:NT], func=AF.Sigmoid, bias=bgcol[:, jc:jc + 1])
                op('act', 'activation', reads=[rGp, res('bgcol')], writes=[gpr], out=gpt[:, 0:NT], in_=bGp[:, 0:NT], func=AF.Sigmoid, bias=bgcol[:, 8 + jc:9 + jc])
                op('dve', 'tensor_tensor', reads=[gar, rA], writes=[gar], out=gat[:, 0:NT], in0=gat[:, 0:NT], in1=bA[:, 0:NT], op=ALU.mult)
                op('dve', 'tensor_tensor', reads=[gpr, rP], writes=[gpr], out=gpt[:, 0:NT], in0=gpt[:, 0:NT], in1=bP[:, 0:NT], op=ALU.mult)
                op('pool', 'tensor_tensor', reads=[gar, gpr], writes=[res('mixT%d' % jc)], out=mixT[:, jc, 0:NT], in0=gat[:, 0:NT], in1=gpt[:, 0:NT], op=ALU.add)
            for _ in range(4):
                w_issue()
        mres = [res('mixT%d' % j) for j in range(8)]
        wu = [[w_get('out'), w_get('out')], [w_get('out'), w_get('out')]]
        if not (split and tile == 0):
            next_x_cb()
        pend = None
        for s, (xt, xr, npr) in enumerate(xl):
            for half in range(2):
                bk, br = bank_get()
                for k in range(8):
                    wv, wr = wu[half][k // 4]
                    mm(bk[0:npr, :], mixT[:, k, s * 128:s * 128 + npr], wv[:, k % 4, :], k == 0, k == 7, [mres[k], wr], [br])
                hs = slice(half * 512, half * 512 + 512)
                op('dve', 'tensor_tensor', reads=[xr, br], writes=[xr], out=xt[0:npr, hs], in0=xt[0:npr, hs], in1=bk[0:npr, :], op=ALU.add)
            hbt, hr = norm_pre(xt, xr, npr)
            if pend is not None:
                norm_post(*pend)
            pend = (s, npr, hbt, hr, g2col, res('g2col'), h2x, h2n)
        norm_post(*pend)
        for _ in range(4):
            w_issue()
        yield 'mix_done'
        if split and tile == 0:
            next_x_cb()
        h2res = [res('%s%d' % (h2n, s)) for s in range(nsub)]
        nseg = 2 if special else 1
        sl_ = NT // nseg
        for ip in range(11):
            wg, rg = w_get('upg')
            wvv, rv = w_get('upv')
            for jj in range(2):
                i = ip * 2 + jj
                cs = slice(jj * 128, jj * 128 + 128)
                cbufs = []
                for which, (wt, wr_) in enumerate(((wg, rg), (wvv, rv))):
                    ch = i + which * 22
                    bk, br = bank_get()
                    for k in range(8):
                        mm(bk[:, 0:NT], wt[:, k, cs], h2x[:, k, 0:NT], k == 0, k == 7, h2res + [wr_], [br])
                    ui = nxt('u_sb', 4)
                    ut, ur = (u_sb[ui], res('u_sb%d' % ui))
                    ci = nxt('c_sb', 4)
                    ct, cr = (c_sb[ci], res('c_sb%d' % ci))
                    uv = ut[:, 0:nseg * (sl_ + 2)].rearrange('p (k c) -> p k c', k=nseg)
                    bv = bk[:, 0:NT].rearrange('p (k c) -> p k c', k=nseg)
                    cv = ct[:, 0:NT].rearrange('p (k c) -> p k c', k=nseg)
                    csr = res('cs_p%d' % ch)
                    if special:
                        op('pool', 'tensor_copy', reads=[res('zero2')], writes=[ur], out=uv[:, 0, 0:2], in_=zero2[:, :])
                        op('pool', 'tensor_copy', reads=[res('cs_s%d' % ch)], writes=[ur], out=uv[:, 1, 0:2], in_=cs_s[:, ch, :])
                    else:
                        op('pool', 'tensor_copy', reads=[csr], writes=[ur], out=uv[:, 0, 0:2], in_=cs_p[:, ch, :])
                    op('act', 'copy', reads=[br], writes=[ur], out=uv[:, :, 2:2 + sl_], in_=bv)
                    if special:
                        op('act', 'copy', reads=[br], writes=[csr], out=cs_p[:, ch, :], in_=bk[:, 14:16])
                        op('act', 'copy', reads=[br], writes=[res('cs_s%d' % ch)], out=cs_s[:, ch, :], in_=bk[:, 30:32])
                    else:
                        op('act', 'copy', reads=[br], writes=[csr], out=cs_p[:, ch, :], in_=bk[:, TT - 2:TT])
                    op('act', 'activation', reads=[br, res('cwcol'), res('cbcol')], writes=[cr], out=cv, in_=bv, func=AF.Identity, scale=cwcol[:, 2, ch:ch + 1], bias=cbcol[:, ch:ch + 1])
                    op('dve', 'scalar_tensor_tensor', reads=[ur, cr, res('cwcol')], writes=[cr], out=cv, in0=uv[:, :, 1:1 + sl_], scalar=cwcol[:, 1, ch:ch + 1], in1=cv, op0=ALU.mult, op1=ALU.add)
                    op('dve', 'scalar_tensor_tensor', reads=[ur, cr, res('cwcol')], writes=[cr], out=cv, in0=uv[:, :, 0:sl_], scalar=cwcol[:, 0, ch:ch + 1], in1=cv, op0=ALU.mult, op1=ALU.add)
                    cbufs.append((ct, cr))
                (cg, cgr), (cvv, cvr) = cbufs
                op('act', 'activation', reads=[cgr], writes=[cgr], out=cg[:, 0:NT], in_=cg[:, 0:NT], func=AF.Silu)
                op('dve', 'tensor_tensor', reads=[cgr, cvr], writes=[res('aT%d' % i)], out=aT[:, i, 0:NT], in0=cg[:, 0:NT], in1=cvv[:, 0:NT], op=ALU.mult)
            w_issue()
            w_issue()
        hoist_cb(0)
        for half in range(2):
            bks = [bank_get() for _ in range(nsub)]
            for kq in range(6):
                wv, wr = w_get('down')
                for kk in range(min(4, 22 - kq * 4)):
                    i = kq * 4 + kk
                    for s, (xt, xr, npr) in enumerate(xl):
                        mm(bks[s][0][0:npr, :], aT[:, i, s * 128:s * 128 + npr], wv[:, kk, :], i == 0, i == 21, [res('aT%d' % i), wr], [bks[s][1]])
                w_issue()
            for s, (xt, xr, npr) in enumerate(xl):
                bk, br = bks[s]
                hs = slice(half * 512, half * 512 + 512)
                op('dve', 'tensor_tensor', reads=[xr, br], writes=[xr], out=xt[0:npr, hs], in0=xt[0:npr, hs], in1=bk[0:npr, :], op=ALU.add)
            hoist_cb(1 + half)
        yield 'pre_tail'
        junk = (pmT[:, 0:2, :].rearrange('p a c -> p (a c)'), [res('pmT0'), res('pmT1')])
        for s, (xt, xr, npr) in enumerate(xl):
            rstd, sr, hbt, hr = rms_rstd(xt, xr, npr, junk)
            op('dve', 'scalar_tensor_tensor', reads=[xr, sr, res('g3b')], writes=[xr], out=xt[0:npr, :], in0=xt[0:npr, :], scalar=rstd, in1=g3b[0:npr, :], op0=ALU.mult, op1=ALU.mult)
            if special:
                dma('sp', y_s, xt[16:32, :], reads=[xr], out=True)
            else:
                r0 = tile * TT + s * 128
                dma('sp', y_p[r0:r0 + 128, :], xt[:, :], reads=[xr], out=True)
        def store_conv_state(cs, csname, dst):
            creads = [res('%s%d' % (csname, c)) for c in range(NCH)]
            bk, br = bank_get()
            for t2 in range(2):
                op('pe', 'transpose', creads + [res('identF')], [br], _a=(bk[0:NCH, t2 * 128:(t2 + 1) * 128], cs[:, :, t2], identF[:, :]))
            op('act', 'copy', reads=[br], writes=[res('ostg')], out=ostg[0:NCH, :, :], in_=bk[0:NCH, 0:256].rearrange('p (t c) -> p t c', t=2))
            for t2 in range(2):
                dma('sp', dst[t2, :].rearrange('(c p) -> c p', p=128), ostg[0:NCH, t2, :], reads=[res('ostg')], out=True)
        if special:
            store_conv_state(cs_s, 'cs_s', o_sconv)
        if tile == NTILE - 1:
            store_conv_state(cs_p, 'cs_p', o_pconv)
    pending = {}
    order = [-1] + list(range(NTILE))
    if _DBG.get('stage') == 'consts':
        order = []

    def mk_cb(nxt_tile):
        def cb():
            if nxt_tile is None:
                return
            pending['x'] = [load_x_sub(nxt_tile, s) for s in range(4)]
        return cb

    def mk_hoist(nxt_tile):
        def hoist(phase):
            if nxt_tile is None or _DBG.get('stage'):
                return
            nx = pending['x']
            args = lambda s_, pr: (s_, nx[s_][2], pr[0], pr[1], g1col, res('g1col'), hT, 'hT')
            if phase == 0:
                pending['pre01'] = [norm_pre(*nx[0]), norm_pre(*nx[1])]
                rope_tables(nxt_tile)
            elif phase == 1:
                p0, p1 = pending.pop('pre01')
                norm_post(*args(0, p0))
                p2 = norm_pre(*nx[2])
                norm_post(*args(1, p1))
                p3 = norm_pre(*nx[3])
                pending['pre23'] = [p2, p3]
            else:
                p2, p3 = pending.pop('pre23')
                norm_post(*args(2, p2))
                norm_post(*args(3, p3))
                pending['pre'] = True
        return hoist

    def drain(g):
        for _ in g:
            pass

    def advance(g, label):
        for v in g:
            if v == label:
                return

    xl = [load_x_sub(-1, 0)] if order else None
    if len(order) >= 2 and not _DBG.get('stage'):
        gS = run_tile(-1, xl, mk_cb(0), lambda phase: None, False, split=True)
        advance(gS, 'mix_done')
        xl0 = pending['x']
        nxt1 = order[2] if len(order) > 2 else None
        g0 = run_tile(0, xl0, mk_cb(nxt1), mk_hoist(nxt1), False, split=True)
        advance(g0, 'mix_done')
        drain(gS)
        advance(g0, 'pre_tail')
        prev = g0
        for idx in range(2, len(order)):
            tile = order[idx]
            nxt_tile = order[idx + 1] if idx + 1 < len(order) else None
            pre = pending.pop('pre', False)
            g = run_tile(tile, pending['x'], mk_cb(nxt_tile), mk_hoist(nxt_tile), pre)
            advance(g, 'projB_done')
            drain(prev)
            advance(g, 'pre_tail')
            prev = g
        drain(prev)
    else:
        for idx in range(len(order)):
            tile = order[idx]
            nxt_tile = order[idx + 1] if idx + 1 < len(order) else None
            pre = pending.pop('pre', False)
            drain(run_tile(tile, xl, mk_cb(nxt_tile), mk_hoist(nxt_tile), pre))
            if nxt_tile is not None:
                xl = pending['x']
    assert _DBG.get('stage') or ws['used'] == len(all_units), (ws['used'], len(all_units))
    S.finish()
    S.emit()
    return nc
_CACHE = {}
_DBG = {}

def kernel(x_prompt, x_sample, cache_swa_k, cache_swa_v, cache_meta_k, cache_meta_v, state_pool, state_conv, meta_tokens, g_norm_mix, w_in, b_gate, sinks, w_attn_o, w_pool_grp, pool_scale, w_pool_o, w_out, g_norm_ffn, w_up, conv_w, conv_b, w_down, g_norm_final):
    if 'nc' not in _CACHE:
        _CACHE['nc'] = build()
    nc = _CACHE['nc']
    f = lambda a: np.ascontiguousarray(np.asarray(a, dtype=np.float32))
    half = 32
    inv32 = (10000.0 ** (-np.arange(half, dtype=np.float64) / half)).astype(np.float32)
    ropec = np.concatenate([inv32, np.zeros(half, np.float32)])
    shared = {'meta': f(meta_tokens), 'g1': f(g_norm_mix), 'w_in': f(w_in), 'b_gate': f(b_gate), 'sinks': f(sinks), 'w_ao': f(w_attn_o), 'w_grp': f(w_pool_grp).reshape(512, 128), 'pscale': f(pool_scale), 'w_po': f(w_pool_o), 'w_out': f(w_out), 'g2': f(g_norm_ffn), 'w_up': f(w_up), 'conv_w': f(conv_w), 'conv_b': f(conv_b), 'w_down': f(w_down), 'g3': f(g_norm_final), 'ropec': ropec}
    xp_, xs_ = (f(x_prompt), f(x_sample))
    ck, cv, mk, mv = (f(cache_swa_k), f(cache_swa_v), f(cache_meta_k), f(cache_meta_v))
    sp_, sc_ = (f(state_pool), f(state_conv))
    in_maps = []
    for b in range(8):
        m = dict(shared)
        m.update({'xp': xp_[b], 'xs': xs_[b], 'cswak': ck[b].reshape(128, 128), 'cswav': cv[b].reshape(128, 128), 'cmetak': mk[b].reshape(16, 128), 'cmetav': mv[b].reshape(16, 128), 'spool': sp_[b], 'sconv': sc_[b]})
        in_maps.append(m)
    res = run_bass_kernel_spmd(nc, in_maps, core_ids=list(range(8)))
    rs = res.results
    st = lambda k, shp: np.stack([np.asarray(r[k], dtype=np.float32).reshape(shp) for r in rs], 0)
    return (st('y_p', (SEQ, D)), st('y_s', (DEC, D)), st('o_swak', (128, 2, 64)), st('o_swav', (128, 2, 64)), st('o_metak', (16, 2, 64)), st('o_metav', (16, 2, 64)), st('o_ppool', (15, 512)), st('o_pconv', (2, 2 * DFF)), st('o_sk', (16, 2, 64)), st('o_sv', (16, 2, 64)), st('o_spool', (15, 512)), st('o_sconv', (2, 2 * DFF)))
```

```python
from contextlib import ExitStack
import math
import numpy as np
import concourse.bass as bass
import concourse.mybir as mybir
from concourse.bass_utils import run_bass_kernel_spmd
F32 = mybir.dt.float32
BF16 = mybir.dt.bfloat16
I32 = mybir.dt.int32
AF = mybir.ActivationFunctionType
ALU = mybir.AluOpType
ENGS = ('pe', 'act', 'dve', 'pool', 'sp')

class Res:
    __slots__ = ('name', 'w', 'r')

    def __init__(self, name):
        self.name = name
        self.w = None
        self.r = []

class Op:
    __slots__ = ('id', 'eng', 'fn', 'deps', 'is_dma', 'signal', 'ticket', 'dsem', 'dval', 'prewait', 'out')

class Sched:
    def __init__(self, nc, n_dma_sems=16):
        self.nc = nc
        self.ops = []
        self.eng_ops = {e: [] for e in ENGS}
        self.n_dma_sems = n_dma_sems
        self.n_dma_sems_eng = {'pool': 64}
        self.dma_count = {e: 0 for e in ENGS}
        self.out_dmas = []

    def op(self, eng, fn, reads=(), writes=(), dma=False, out=False, force=()):
        o = Op()
        o.id = len(self.ops)
        o.eng = eng
        o.fn = fn
        o.is_dma = dma
        o.signal = False
        o.ticket = None
        o.out = out
        o.prewait = None
        deps = set()
        for r in reads:
            if r.w is not None:
                deps.add(r.w)
        for w in writes:
            if w.w is not None:
                deps.add(w.w)
            deps.update(w.r)
        for r in reads:
            r.r.append(o.id)
        for w in writes:
            w.w = o.id
            w.r = []
        deps.discard(o.id)
        if eng == 'pe' and (not dma):
            deps = {d for d in deps if not (self.ops[d].eng == 'pe' and (not self.ops[d].is_dma))}
        deps |= set(force)
        o.deps = deps
        if dma:
            k = self.dma_count[eng]
            self.dma_count[eng] += 1
            nds = self.n_dma_sems_eng.get(eng, self.n_dma_sems)
            o.dsem = (eng, k % nds)
            o.dval = 16 * (k // nds + 1)
            if k >= nds:
                o.prewait = (o.dsem, o.dval - 16)
            if out:
                self.out_dmas.append(o.id)
        self.ops.append(o)
        self.eng_ops[eng].append(o)
        return o

    def finish(self):
        o = Op()
        o.id = len(self.ops)
        o.eng = 'sp'
        o.fn = None
        o.is_dma = False
        o.signal = False
        o.ticket = None
        o.out = False
        o.prewait = None
        o.deps = set(self.out_dmas)
        self.ops.append(o)
        self.eng_ops['sp'].append(o)

    def emit(self):
        nc = self.nc
        pos = {}
        for e in ENGS:
            for i, o in enumerate(self.eng_ops[e]):
                pos[o.id] = i
        for o in self.ops:
            best = {}
            dmas = set()
            for d in o.deps:
                od = self.ops[d]
                if od.is_dma:
                    dmas.add(d)
                elif od.eng not in best or pos[d] > pos[best[od.eng]]:
                    best[od.eng] = d
            for d in best.values():
                self.ops[d].signal = True
            o.deps = dmas | set(best.values())
        for e in ENGS:
            t = 0
            for o in self.eng_ops[e]:
                if o.signal:
                    t += 1
                    o.ticket = t
        with ExitStack() as st:
            esem = {e: st.enter_context(nc.semaphore('es_' + e)) for e in ENGS}
            dsem = {}
            for e in ENGS:
                if self.dma_count[e] > 0:
                    for i in range(min(self.n_dma_sems_eng.get(e, self.n_dma_sems), self.dma_count[e])):
                        dsem[e, i] = st.enter_context(nc.semaphore('ds_%s_%d' % (e, i)))
            block = st.enter_context(nc.Block())

            def mk(e):

                def body(eng):
                    known = {}

                    def wait(key, sem, val):
                        if known.get(key, 0) < val:
                            eng.wait_ge(sem, val)
                            known[key] = val
                    for o in self.eng_ops[e]:
                        if o.prewait is not None:
                            wait(o.prewait[0], dsem[o.prewait[0]], o.prewait[1])
                        ws = {}
                        for d in o.deps:
                            od = self.ops[d]
                            if od.is_dma:
                                key, val = (od.dsem, od.dval)
                            else:
                                key, val = (od.eng, od.ticket)
                            if ws.get(key, 0) < val:
                                ws[key] = val
                        for key, val in ws.items():
                            sem = dsem[key] if isinstance(key, tuple) else esem[key]
                            wait(key, sem, val)
                        if o.fn is None:
                            continue
                        ins = o.fn(eng)
                        if o.is_dma:
                            ins.then_inc(dsem[o.dsem], 16)
                        elif o.signal:
                            ins.then_inc(esem[e], 1)
                return body
            block.tensor(mk('pe'))
            block.scalar(mk('act'))
            block.vector(mk('dve'))
            block.gpsimd(mk('pool'))
            block.sync(mk('sp'))
D = 1024
SEQ = 4096
NTILE = 8
TT = 512
NMETA = 16
DEC = 16
DFF = 2816
NCH = 44
NSLOT = 8
POOLW = (2, 4, 8, 16)
EPS = 1e-06

def build():
    nc = bass.Bass('TRN2', target_bir_lowering=False)
    S = Sched(nc)

    def din(name, shape):
        return nc.dram_tensor(name, list(shape), F32, kind='ExternalInput').ap()

    def dout(name, shape):
        return nc.dram_tensor(name, list(shape), F32, kind='ExternalOutput').ap()
    xp = din('xp', [SEQ, D])
    xs = din('xs', [DEC, D])
    cswak = din('cswak', [128, 128])
    cswav = din('cswav', [128, 128])
    cmetak = din('cmetak', [16, 128])
    cmetav = din('cmetav', [16, 128])
    spool = din('spool', [15, 512])
    sconv = din('sconv', [2, 2 * DFF])
    meta = din('meta', [NMETA, D])
    g1 = din('g1', [D])
    w_in = din('w_in', [D, 3840])
    b_gate = din('b_gate', [2048])
    sinks = din('sinks', [16])
    w_ao = din('w_ao', [D, D])
    w_grp = din('w_grp', [512, 128])
    pscale = din('pscale', [512])
    w_po = din('w_po', [512, D])
    w_out = din('w_out', [D, D])
    g2 = din('g2', [D])
    w_up = din('w_up', [D, 2 * DFF])
    conv_w = din('conv_w', [3, 2 * DFF])
    conv_b = din('conv_b', [2 * DFF])
    w_down = din('w_down', [DFF, D])
    g3 = din('g3', [D])
    ropec = din('ropec', [64])
    y_p = dout('y_p', [SEQ, D])
    y_s = dout('y_s', [DEC, D])
    o_swak = dout('o_swak', [128, 128])
    o_swav = dout('o_swav', [128, 128])
    o_metak = dout('o_metak', [16, 128])
    o_metav = dout('o_metav', [16, 128])
    o_ppool = dout('o_ppool', [15, 512])
    o_pconv = dout('o_pconv', [2, 2 * DFF])
    o_sk = dout('o_sk', [16, 128])
    o_sv = dout('o_sv', [16, 128])
    o_spool = dout('o_spool', [15, 512])
    o_sconv = dout('o_sconv', [2, 2 * DFF])

    def scr(name, shape):
        return nc.dram_tensor(name, list(shape), BF16, kind='Internal').ap()
    s_win = scr('s_win', [D, 3840])
    s_ao = scr('s_ao', [D, D])
    s_grp = scr('s_grp', [512, 128])
    s_po = scr('s_po', [512, D])
    s_out = scr('s_out', [D, D])
    s_up = scr('s_up', [D, 2 * DFF])
    s_down = scr('s_down', [DFF, D])

    def sb(name, shape, dt=F32):
        return nc.alloc_sbuf_tensor(name, list(shape), dt)
    identF = sb('identF', [128, 128])
    ident = sb('ident', [128, 128], BF16)
    identf = sb('identf', [128, 128])
    ones = sb('ones', [128, 128], BF16)
    g1col = sb('g1col', [128, 8])
    g2col = sb('g2col', [128, 8])
    g3b = sb('g3b', [128, D])
    bgcol = sb('bgcol', [128, 16])
    pscol = sb('pscol', [128, 4])
    cwcol = sb('cwcol', [128, 3, NCH])
    cbcol = sb('cbcol', [128, NCH])
    ropecb = sb('ropecb', [128, 64])
    sk_f = sb('sk_f', [1, 16])
    sk_e = sb('sk_e', [1, 16])
    esrow = sb('esrow', [1, 2, 512], BF16)
    invcnt = sb('invcnt', [128, 4, 16])
    posf = sb('posf', [128, 4])
    rp_u = sb('rp_u', [128, 4, 32])
    rp_i = sb('rp_i', [128, 4, 32], I32)
    rp_f = sb('rp_f', [128, 4, 32])
    rp_r = sb('rp_r', [128, 4, 32])
    cos_t = sb('cos_t', [128, 4, 32])
    sin_t = sb('sin_t', [128, 4, 32])
    nsin_t = sb('nsin_t', [128, 4, 32])
    wslot = [sb('wslot%d' % i, [128, 2048], BF16) for i in range(NSLOT)]
    NX = 8
    x_sb = [sb('x_sb%d' % i, [128, D]) for i in range(NX)]
    hb = [sb('hb%d' % i, [128, D], BF16) for i in range(2)]
    st_ss = [sb('st_ss%d' % i, [128, 4]) for i in range(2)]
    hT = sb('hT', [128, 8, TT], BF16)
    h2T = sb('h2T', [128, 8, TT], BF16)
    h2Ts = sb('h2Ts', [128, 8, 32], BF16)
    kf = [sb('kf%d' % i, [128, 128]) for i in range(2)]
    kbt = [sb('kbt%d' % i, [128, 128], BF16) for i in range(4)]
    vf = [sb('vf%d' % i, [128, 128]) for i in range(2)]
    QO = sb('QO', [128, 8, TT], BF16)
    KT = sb('KT', [128, 16 + 1024], BF16)
    VR = sb('VR', [128, 8, 256], BF16)
    VmLo = sb('VmLo', [128, 256], BF16)
    VmHi = sb('VmHi', [128, 256], BF16)
    KTs = sb('KTs', [128, 16], BF16)
    KTcm = sb('KTcm', [128, 16], BF16)
    KTcs = sb('KTcs', [128, 128], BF16)
    Vs = sb('Vs', [128, 256], BF16)
    Vcm = sb('Vcm', [128, 256], BF16)
    Vcs = sb('Vcs', [128, 256], BF16)
    cst_f = sb('cst_f', [128, 128])
    cst_b = sb('cst_b', [128, 128], BF16)
    pT = sb('pT', [128, 4, 15 + TT])
    pS = [sb('pS%d' % i, [128, 4, 31]) for i in range(2)]
    ptmp = sb('ptmp', [128, 16])
    pmT = sb('pmT', [128, 4, TT], BF16)
    poolT = sb('poolT', [128, 4, TT], BF16)
    ptok = sb('ptok', [128, 512])
    EA = [sb('EA%d' % i, [128, 512], BF16) for i in range(4)]
    EBm = [[sb('EBm%d_%d' % (n, i), [128, 512], BF16) for i in range(2)] for n in range(2)]
    VB = sb('VB', [128, 9, 256], BF16)
    rden = [sb('rden%d' % i, [128, 256]) for i in range(2)]
    ga = [sb('ga%d' % i, [128, TT]) for i in range(2)]
    gp = [sb('gp%d' % i, [128, TT]) for i in range(2)]
    mixT = sb('mixT', [128, 8, TT], BF16)
    u_sb = [sb('u_sb%d' % i, [128, TT + 16]) for i in range(4)]
    ptA = [u_sb[0]]
    ptB = [u_sb[1]]
    c_sb = [sb('c_sb%d' % i, [128, TT]) for i in range(4)]
    aT = sb('aT', [128, 22, TT], BF16)
    qb = aT[:, 0:8, :].rearrange('p (s a) c -> p s (a c)', s=4)
    rt1 = [c_sb[0], c_sb[1]]
    rt2 = [c_sb[2], c_sb[3]]
    cs_p = sb('cs_p', [128, NCH, 2])
    cs_s = sb('cs_s', [128, NCH, 2])
    zero2 = sb('zero2', [128, 2])
    epsc = sb('epsc', [128, 1])
    stg = ga[0][:, :].rearrange('p (b c) -> p b c', b=4)
    ostg = ptok[:, 0:256].rearrange('p (t c) -> p t c', t=2)
    banks = [nc.alloc_psum_tensor('bank%d' % i, [128, 512], F32) for i in range(8)]
    trb = [banks[6 + i][:, :].bitcast(BF16).rearrange('p (a b) -> p a b', a=8) for i in range(2)]
    R = {}

    def res(name):
        if name not in R:
            R[name] = Res(name)
        return R[name]
    R['stg'] = res('ga0')
    R['ptA0'] = res('u_sb0')
    R['ptB0'] = res('u_sb1')
    R['trb0'] = res('bank6')
    R['trb1'] = res('bank7')
    for _i in range(2):
        R['rt1_%d' % _i] = res('c_sb%d' % _i)
        R['rt2_%d' % _i] = res('c_sb%d' % (2 + _i))
    R['ostg'] = res('ptok')
    rot = {}

    def nxt(name, n):
        i = rot.get(name, 0)
        rot[name] = i + 1
        return i % n

    bank_lru = list(range(8))

    def bank_get(att=False):
        lim = 8 if att else 6
        i = next(b_ for b_ in bank_lru if b_ < lim)
        bank_lru.remove(i)
        bank_lru.append(i)
        return (banks[i], res('bank%d' % i))

    def trb_get():
        i = nxt('trb', 2)
        bank_lru.remove(6 + i)
        bank_lru.append(6 + i)
        return (trb[i], res('trb%d' % i))

    def op(eng, name, reads=(), writes=(), _a=(), _force=(), **kw):

        def fn(e, name=name, _a=_a, kw=kw):
            return getattr(e, name)(*_a, **kw)
        return S.op(eng, fn, reads=reads, writes=writes, force=_force)

    def dma(eng, out_ap, in_ap, reads=(), writes=(), out=False, slow=False):
        if slow:
            return S.op(eng, lambda e: e.dma_start(out=out_ap, in_=in_ap, allow_slow_non_contiguous=True), reads=reads, writes=writes, dma=True, out=out)
        return S.op(eng, lambda e: e.dma_start(out=out_ap, in_=in_ap), reads=reads, writes=writes, dma=True, out=out)


    def vdup_out(ap):
        return ap.rearrange('p (n u d) -> p n u d', n=2, u=2)

    def vdup_in(ap, rows):
        return ap.rearrange('p (n d) -> p n d', n=2).unsqueeze(2).to_broadcast([rows, 2, 2, 64])
    def vb_slot(c):
        return c % 8 if c % 8 != 7 else 7 + (c // 8) % 2

    pe_last = {}

    def mm(out_ap, lhsT, rhs, start, stop, reads, writes, tp=None):
        kb = tp[0] if tp is not None else 0
        K = lhsT.shape[0]
        groups = set(range(kb // 32, (kb + K - 1) // 32 + 1))
        force = []
        for w in writes:
            prev = pe_last.get(w.name)
            if prev is not None and not (prev[1] & groups):
                force.append(prev[0])
        kw = {} if tp is None else {'tile_position': tp}
        o = op('pe', 'matmul', reads, writes, _a=(out_ap,), _force=force, lhsT=lhsT, rhs=rhs, start=start, stop=stop, **kw)
        for w in writes:
            pe_last[w.name] = (o.id, groups)

    def tr(out_ap, in_ap, n, reads, writes):
        op('pe', 'transpose', reads, writes, _a=(out_ap, in_ap, ident[0:n, 0:n]))

    def precast(dst, src, rows, tag):
        nblk = rows // 128
        for k in range(nblk):
            dma('pool', dst[k * 128:(k + 1) * 128, :], src[k * 128:(k + 1) * 128, :], writes=[res('%s_%d' % (tag, k))])
    precast(s_win, w_in, D, 'sc_win')
    op('pool', 'memset', writes=[res('stg')], _a=(stg[:, :, :], 0.0))
    dma('sp', stg[0:8, 0, :], g1.rearrange('(k p) -> k p', p=128), reads=[res('stg')], writes=[res('stgd_0')])
    dma('sp', stg[8:16, 0, :], g2.rearrange('(k p) -> k p', p=128), reads=[res('stg')], writes=[res('stgd_1')])
    dma('sp', stg[16:32, 0, :], b_gate.rearrange('(k p) -> k p', p=128), reads=[res('stg')], writes=[res('stgd_2')])
    dma('sp', stg[32:36, 0, :], pscale.rearrange('(k p) -> k p', p=128), reads=[res('stg')], writes=[res('stgd_3')])
    dma('sp', stg[36:80, 0, :], conv_b.rearrange('(k p) -> k p', p=128), reads=[res('stg')], writes=[res('stgd_4')])
    cw_rows = conv_w.rearrange('j (c p) -> (j c) p', p=128)
    dma('sp', stg[0:128, 1, :], cw_rows[0:128, :], reads=[res('stg')], writes=[res('stgd_5')])
    dma('sp', stg[0:4, 2, :], cw_rows[128:132, :], reads=[res('stg')], writes=[res('stgd_6')])
    dma('sp', stg[32:120, 2, :], sconv.rearrange('t (c p) -> (t c) p', p=128), reads=[res('stg')], writes=[res('stgd_7')])
    for g in range(4):
        dma('sp', stg[g * 32:g * 32 + 15, 3, :], spool[:, g * 128:(g + 1) * 128], reads=[res('stg')], writes=[res('stgd_sp%d' % g)])
    dma('sp', g3b[:, :], g3.partition_broadcast(128), writes=[res('g3b')])
    dma('sp', ropecb[:, :], ropec.partition_broadcast(128), writes=[res('ropecb')])
    dma('sp', sk_f[:, :], sinks.rearrange('(o s) -> o s', o=1), writes=[res('sk_f')])
    op('pool', 'iota', writes=[res('identf')], _a=(identf[:, :],), pattern=[[1, 128]], base=0, channel_multiplier=-1, allow_small_or_imprecise_dtypes=True)
    op('dve', 'tensor_single_scalar', reads=[res('identf')], writes=[res('ident')], out=ident[:, :], in_=identf[:, :], scalar=0.0, op=ALU.is_equal)
    for _i in range(8):
        op('dve', 'memset', writes=[res('bank%d' % _i)], _a=(banks[_i][:, :], 0.0))
    op('dve', 'tensor_single_scalar', reads=[res('identf')], writes=[res('identF')], out=identF[:, :], in_=identf[:, :], scalar=0.0, op=ALU.is_equal)
    for _b in range(4):
        op('pe', 'transpose', [res('stg'), res('identF')] + [res('stgd_%d' % _j) for _j in range(8)] + [res('stgd_sp%d' % _g) for _g in range(4)], [res('bank0')], _a=(banks[0][:, _b * 128:(_b + 1) * 128], stg[:, _b, :], identF[:, :]))
    _T = banks[0]
    op('act', 'copy', reads=[res('bank0')], writes=[res('g1col')], out=g1col[:, :], in_=_T[:, 0:8])
    op('act', 'copy', reads=[res('bank0')], writes=[res('g2col')], out=g2col[:, :], in_=_T[:, 8:16])
    op('act', 'copy', reads=[res('bank0')], writes=[res('bgcol')], out=bgcol[:, :], in_=_T[:, 16:32])
    op('act', 'copy', reads=[res('bank0')], writes=[res('pscol')], out=pscol[:, :], in_=_T[:, 32:36])
    op('act', 'copy', reads=[res('bank0')], writes=[res('cbcol')], out=cbcol[:, :], in_=_T[:, 36:80])
    _cwf = cwcol[:, :, :].rearrange('p j c -> p (j c)')
    op('dve', 'tensor_copy', reads=[res('bank0')], writes=[res('cwcol')], out=_cwf[:, 0:128], in_=_T[:, 128:256])
    op('dve', 'tensor_copy', reads=[res('bank0')], writes=[res('cwcol')], out=_cwf[:, 128:132], in_=_T[:, 256:260])
    op('dve', 'tensor_copy', reads=[res('bank0')], writes=[res('cs_s%d' % c) for c in range(NCH)], out=cs_s[:, :, :], in_=_T[:, 256 + 32:256 + 120].rearrange('p (t c) -> p c t', t=2))
    op('dve', 'tensor_copy', reads=[res('bank0')], writes=[res('pS1_%d' % g) for g in range(4)], out=pS[1][:, :, 0:15], in_=_T[:, 384:512].rearrange('p (g r) -> p g r', r=32)[:, :, 0:15])
    op('pool', 'memset', writes=[res('ones')], _a=(ones[:, :], 1.0))
    op('pool', 'memset', writes=[res('zero2')], _a=(zero2[:, :], 0.0))
    op('pool', 'memset', writes=[res('epsc')], _a=(epsc[:, :], EPS))
    op('pool', 'memset', writes=[res('cs_p%d' % c) for c in range(NCH)], _a=(cs_p[:, :, :], 0.0))
    op('pool', 'memset', writes=[res('pS0_%d' % g) for g in range(4)], _a=(pS[0][:, :, :], 0.0))
    op('pool', 'memset', writes=[res('VmLo')], _a=(VmLo[:, :], 0.0))
    op('pool', 'memset', writes=[res('VmHi')], _a=(VmHi[:, :], 0.0))
    op('act', 'activation', reads=[res('sk_f')], writes=[res('sk_e')], out=sk_e[:, :], in_=sk_f[:, :], func=AF.Exp)
    for n in range(2):
        op('dve', 'tensor_copy', reads=[res('sk_e')], writes=[res('esrow')], out=esrow[0:1, n, :].rearrange('o (g q) -> o g q', g=8), in_=sk_e[0:1, n * 8:(n + 1) * 8].unsqueeze(2).to_broadcast([1, 8, 64]))
    op('pool', 'memset', writes=[res('VBm%d' % r) for r in range(9)] + [res('VB%d' % r) for r in range(9)], _a=(VB[:, :, :], 0.0))
    for n in range(2):
        for r in range(2):
            dma('sp', EBm[n][r][80:81, :], esrow[0:1, n, :], reads=[res('esrow')], writes=[res('EBm%d_%d_sink' % (n, r))])
    op('pool', 'iota', writes=[res('invcnt')], _a=(invcnt[:, 0, :],), pattern=[[1, 16]], base=1, channel_multiplier=0, allow_small_or_imprecise_dtypes=True)
    for g in (1, 2, 3):
        op('dve', 'tensor_scalar', reads=[res('invcnt')], writes=[res('invcnt')], out=invcnt[:, g, :], in0=invcnt[:, 0, :], scalar1=float(POOLW[g]), scalar2=None, op0=ALU.min)
    op('dve', 'tensor_scalar', reads=[res('invcnt')], writes=[res('invcnt')], out=invcnt[:, 0, :], in0=invcnt[:, 0, :], scalar1=float(POOLW[0]), scalar2=None, op0=ALU.min)
    op('dve', 'reciprocal', reads=[res('invcnt')], writes=[res('invcnt')], out=invcnt[:, :, :], in_=invcnt[:, :, :])

    def load_cast(dst_bf, dst_rows, src, nrows, rname, transpose_to=None):
        r0 = dst_rows
        dma('sp', cst_f[r0:r0 + nrows, :], src, writes=[res('cst_f')])
        if transpose_to is None:
            op('dve', 'tensor_copy', reads=[res('cst_f')], writes=[res(rname)], out=vdup_out(dst_bf[r0:r0 + nrows, :]), in_=vdup_in(cst_f[r0:r0 + nrows, :], nrows))
        else:
            op('dve', 'tensor_copy', reads=[res('cst_f')], writes=[res('cst_b')], out=cst_b[r0:r0 + nrows, :], in_=cst_f[r0:r0 + nrows, :])
            tb, tr = trb_get()
            op('pe', 'transpose', reads=[res('cst_b'), res('ident')], writes=[tr], _a=(tb[:, 0, 0:nrows], cst_b[0:nrows, :], ident[0:nrows, 0:nrows]))
            op('act', 'copy', reads=[tr], writes=[res(rname)], out=transpose_to[:, 0:nrows], in_=tb[:, 0, 0:nrows])
    load_cast(Vcs, 0, cswav, 128, 'Vcs')
    load_cast(Vcm, 64, cmetav, 16, 'Vcm')
    load_cast(None, 0, cswak, 128, 'KTcs', transpose_to=KTcs)
    load_cast(None, 0, cmetak, 16, 'KTcm', transpose_to=KTcm)
    precast(s_grp, w_grp, 512, 'sc_grp')
    precast(s_ao, w_ao, D, 'sc_ao')
    precast(s_po, w_po, 512, 'sc_po')
    precast(s_out, w_out, D, 'sc_out')

    def kp(ap):
        return ap.rearrange('(k p) c -> p k c', p=128)

    def tile_units():
        u = []
        for c0, ncol in ((1024, 256), (0, 512), (512, 512)):
            for kh in range(2):
                u.append(('qkv', kp(s_win)[:, kh * 4:kh * 4 + 4, c0:c0 + ncol], (4, ncol), ['sc_win_%d' % k for k in range(kh * 4, kh * 4 + 4)]))
        for half in range(2):
            u.append(('p', kp(s_win)[:, :, 1280 + half * 256:1280 + half * 256 + 256], (8, 256), ['sc_win_%d' % k for k in range(8)]))
        u.append(('grp', s_grp.rearrange('(g c) d -> c g d', c=128), (4, 128), ['sc_grp_%d' % k for k in range(4)]))
        for jp in range(4):
            u.append(('ao', kp(s_ao)[:, :, jp * 256:(jp + 1) * 256], (8, 256), ['sc_ao_%d' % k for k in range(8)]))
            u.append(('po', kp(s_po)[:, :, jp * 256:(jp + 1) * 256], (4, 256), ['sc_po_%d' % k for k in range(4)]))
            u.append(('ga', kp(s_win)[:, :, 1792 + jp * 256:1792 + jp * 256 + 256], (8, 256), ['sc_win_%d' % k for k in range(8)]))
            u.append(('gp', kp(s_win)[:, :, 2816 + jp * 256:2816 + jp * 256 + 256], (8, 256), ['sc_win_%d' % k for k in range(8)]))
        for half in range(2):
            for kh in range(2):
                u.append(('out', kp(s_out)[:, kh * 4:kh * 4 + 4, half * 512:half * 512 + 512], (4, 512), ['sc_out_%d' % k for k in range(kh * 4, kh * 4 + 4)]))
        for ip in range(11):
            u.append(('upg', kp(s_up)[:, :, ip * 256:(ip + 1) * 256], (8, 256), ['sc_up_%d' % k for k in range(8)]))
            u.append(('upv', kp(s_up)[:, :, DFF + ip * 256:DFF + (ip + 1) * 256], (8, 256), ['sc_up_%d' % k for k in range(8)]))
        for half in range(2):
            for kq in range(6):
                k0, k1 = (kq * 4, min(kq * 4 + 4, 22))
                u.append(('down', kp(s_down)[:, k0:k1, half * 512:half * 512 + 512], (k1 - k0, 512), ['sc_down_%d' % k for k in range(k0, k1)]))
        return u
    _tu = tile_units()
    _nmix = sum(1 for u_ in _tu if u_[0] not in ('upg', 'upv', 'down'))
    mix_units, ffn_units = _tu[:_nmix], _tu[_nmix:]
    assert all(u_[0] in ('upg', 'upv', 'down') for u_ in ffn_units)
    if NTILE >= 1:
        all_units = mix_units + mix_units + ffn_units + ffn_units
        for _t in range(1, NTILE):
            all_units = all_units + _tu
    else:
        all_units = list(_tu)
    ws = {'loaded': 0, 'used': 0}

    def w_issue():
        i = ws['loaded']
        if i >= len(all_units):
            return
        kind, src, (a, b), deps = all_units[i]
        sl = i % NSLOT
        dst = wslot[sl][:, 0:a * b].rearrange('p (k c) -> p k c', k=a)
        dma('sp', dst, src, reads=[res(d) for d in deps], writes=[res('wslot%d' % sl)])
        ws['loaded'] += 1

    def w_get(kind):
        i = ws['used']
        k, src, (a, b), deps = all_units[i]
        assert k == kind, (k, kind)
        assert i < ws['loaded']
        sl = i % NSLOT
        ws['used'] += 1
        return (wslot[sl][:, 0:a * b].rearrange('p (k c) -> p k c', k=a), res('wslot%d' % sl))
    for _ in range(NSLOT):
        w_issue()
    xrot = {'i': 0}

    def load_x_sub(tile, s):
        i = xrot['i'] % NX
        xrot['i'] += 1
        if tile < 0:
            dma('sp', x_sb[i][0:16, :], meta, writes=[res('x_sb%d' % i)])
            dma('sp', x_sb[i][16:32, :], xs, writes=[res('x_sb%d' % i)])
            return (x_sb[i], res('x_sb%d' % i), 32)
        r0 = tile * TT + s * 128
        dma('sp', x_sb[i][:, :], xp[r0:r0 + 128, :], writes=[res('x_sb%d' % i)])
        return (x_sb[i], res('x_sb%d' % i), 128)

    def rms_rstd(xt, xr, npr, junk=None):
        i = nxt('st_ss', 2)
        stt, sr = (st_ss[i], res('st_ss%d' % i))
        if junk is None:
            j = nxt('hb', 2)
            hbt, hr = (hb[j], res('hb%d' % j))
            jw = [hr]
            jt = hbt
        else:
            hbt, hr = (None, None)
            jt, jw = junk
        op('act', 'activation', reads=[xr], writes=jw + [sr], out=jt[0:npr, :], in_=xt[0:npr, :], func=AF.Square, accum_out=stt[0:npr, 0:1])
        op('act', 'activation', reads=[sr, res('epsc')], writes=[sr], out=stt[0:npr, 2:3], in_=stt[0:npr, 0:1], func=AF.Sqrt, scale=1.0 / D, bias=epsc[0:npr, 0:1])
        op('dve', 'reciprocal', reads=[sr], writes=[sr], out=stt[0:npr, 3:4], in_=stt[0:npr, 2:3])
        return (stt[0:npr, 3:4], sr, hbt, hr)

    def norm_pre(xt, xr, npr):
        rstd, sr, hbt, hr = rms_rstd(xt, xr, npr)
        op('dve', 'tensor_scalar', reads=[xr, sr], writes=[hr], out=hbt[0:npr, :], in0=xt[0:npr, :], scalar1=rstd, scalar2=None, op0=ALU.mult)
        return (hbt, hr)

    def norm_post(s, npr, hbt, hr, gcol, gres, dst, dname):
        tb, tr = trb_get()
        for j in range(8):
            op('pe', 'transpose', reads=[hr, res('ident')], writes=[tr], _a=(tb[:, j, 0:npr], hbt[0:npr, j * 128:(j + 1) * 128], ident[0:npr, 0:npr]))
        op('dve', 'tensor_tensor', reads=[tr, gres], writes=[res('%s%d' % (dname, s))], out=dst[:, :, s * 128:s * 128 + npr], in0=tb[:, :, 0:npr], in1=gcol[:, :].unsqueeze(2).to_broadcast([128, 8, npr]), op=ALU.mult)

    def norm_T(xl, gcol, gres, dst=None, dname='hT'):
        dst = hT if dst is None else dst
        for s, (xt, xr, npr) in enumerate(xl):
            hbt, hr = norm_pre(xt, xr, npr)
            norm_post(s, npr, hbt, hr, gcol, gres, dst, dname)

    def rope_tables(tile):
        nsub = 1 if tile < 0 else 4
        if tile < 0:
            op('pool', 'iota', writes=[res('posf')], _a=(posf[:, 0:1],), pattern=[[1, 1]], base=0, channel_multiplier=1, allow_small_or_imprecise_dtypes=True)
            op('dve', 'tensor_single_scalar', reads=[res('posf')], writes=[res('posf')], out=posf[:, 1:2], in_=posf[:, 0:1], scalar=16.0, op=ALU.is_ge)
            op('dve', 'scalar_tensor_tensor', reads=[res('posf')], writes=[res('posf')], out=posf[:, 0:1], in0=posf[:, 1:2], scalar=1024.0, in1=posf[:, 0:1], op0=ALU.mult, op1=ALU.add)
        else:
            op('pool', 'iota', writes=[res('posf')], _a=(posf[:, 0:4],), pattern=[[128, 4]], base=NMETA + tile * TT, channel_multiplier=1, allow_small_or_imprecise_dtypes=True)
        R_ = [res('posf'), res('ropecb'), res('rp_u'), res('rp_r'), res('rp_i'), res('rp_f')]
        for s in range(nsub):
            op('dve', 'tensor_scalar', reads=R_, writes=[res('rp_u')], out=rp_u[:, s, :], in0=ropecb[:, 0:32], scalar1=posf[:, s:s + 1], scalar2=None, op0=ALU.mult)
        sl = slice(0, nsub)
        C1 = 6.28125
        C2 = float(np.float32(2.0 * math.pi - C1))
        C3 = float(np.float32(2.0 * math.pi - C1 - C2))
        I2P = float(np.float32(1.0 / (2.0 * math.pi)))

        def reduce_(t, tres, consts):
            op('dve', 'tensor_scalar', reads=R_, writes=[res('rp_f')], out=rp_f[:, sl, :], in0=t[:, sl, :], scalar1=I2P, scalar2=None, op0=ALU.mult)
            op('dve', 'tensor_copy', reads=R_, writes=[res('rp_i')], out=rp_i[:, sl, :], in_=rp_f[:, sl, :])
            op('dve', 'tensor_copy', reads=R_, writes=[res('rp_f')], out=rp_f[:, sl, :], in_=rp_i[:, sl, :])
            for c_ in consts:
                op('dve', 'scalar_tensor_tensor', reads=R_, writes=[tres], out=t[:, sl, :], in0=rp_f[:, sl, :], scalar=-c_, in1=t[:, sl, :], op0=ALU.mult, op1=ALU.add)
        reduce_(rp_u, res('rp_u'), (C1, C2, C3))
        reduce_(rp_u, res('rp_u'), (C1, C2))
        op('act', 'activation', reads=[res('rp_u')], writes=[res('sin_t')], out=sin_t[:, sl, :], in_=rp_u[:, sl, :], func=AF.Sin, scale=0.9999998)
        op('dve', 'tensor_scalar', reads=R_, writes=[res('rp_r')], out=rp_r[:, sl, :], in0=rp_u[:, sl, :], scalar1=float(np.float32(math.pi / 2.0)), scalar2=None, op0=ALU.add)
        reduce_(rp_r, res('rp_r'), (C1, C2))
        op('act', 'activation', reads=[res('rp_r')], writes=[res('cos_t')], out=cos_t[:, sl, :], in_=rp_r[:, sl, :], func=AF.Sin, scale=0.9999998)
        op('dve', 'tensor_scalar', reads=[res('sin_t')], writes=[res('nsin_t')], out=nsin_t[:, sl, :], in0=sin_t[:, sl, :], scalar1=-1.0, scalar2=None, op0=ALU.mult)

    def bc(tab, s, npr, nh):
        return tab[0:npr, s, :].unsqueeze(1).to_broadcast([npr, nh, 32])

    def rope_bank(bk, br, s, npr, nh, col0, out_ap, out_res, out_eng):
        i = nxt('rt', 2)
        t1, t2 = (rt1[i], rt2[i])
        r1, r2 = (res('rt1_%d' % i), res('rt2_%d' % i))
        v4 = bk[0:npr, col0:col0 + nh * 64].rearrange('p (h two d) -> p h two d', two=2, d=32)
        t1v = t1[0:npr, 0:nh * 64].rearrange('p (h two d) -> p h two d', two=2, d=32)
        t2v = t2[0:npr, 0:nh * 64].rearrange('p (h two d) -> p h two d', two=2, d=32)
        tabs = [res('cos_t'), res('sin_t'), res('nsin_t')]
        op('dve', 'tensor_tensor', reads=[br] + tabs, writes=[r1], out=t1v, in0=v4, in1=cos_t[0:npr, s, :].unsqueeze(1).unsqueeze(1).to_broadcast([npr, nh, 2, 32]), op=ALU.mult)
        op('dve', 'tensor_tensor', reads=[br] + tabs, writes=[r2], out=t2v[:, :, 0, :], in0=v4[:, :, 1, :], in1=bc(nsin_t, s, npr, nh), op=ALU.mult)
        op('dve', 'tensor_tensor', reads=[br] + tabs, writes=[r2], out=t2v[:, :, 1, :], in0=v4[:, :, 0, :], in1=bc(sin_t, s, npr, nh), op=ALU.mult)
        op(out_eng, 'tensor_tensor', reads=[r1, r2], writes=(out_res if isinstance(out_res, list) else [out_res]), out=out_ap, in0=t1[0:npr, 0:nh * 64].rearrange('p (h e) -> p h e', e=64), in1=t2[0:npr, 0:nh * 64].rearrange('p (h e) -> p h e', e=64), op=ALU.add)

    def attn_scores(n_q, q0, qres, key_banks):
        ncol = 8 * n_q
        kbs = []
        for kb in key_banks:
            if isinstance(kb, tuple) and kb[0] == 'M':
                kbs.append((kb[1], (kb[2], kb[3])))
            elif isinstance(kb, tuple):
                kbs.append((kb[1], 'E'))
            else:
                kbs.append((kb, None))
        bks = [[bank_get(att=True) for _ in kbs] for n in range(2)]
        for bi, (segs, mg) in enumerate(kbs):
            for kt_ap, kres, v_ap, vres, rb, nk in segs:
                for n in range(2):
                    bk, br = bks[n][bi]
                    mm(bk[rb:rb + nk, 0:ncol].rearrange('p (g q) -> p g q', g=8), kt_ap[n * 64:(n + 1) * 64, :], QO[n * 64:(n + 1) * 64, :, q0:q0 + n_q], True, True, [kres, qres], [br], tp=(n * 64, rb))
        eb_all = []
        for n in range(2):
            ebufs = []
            for bi, (segs, mg) in enumerate(kbs):
                bk, br = bks[n][bi]
                lo = min((sg[4] for sg in segs))
                hi = max((sg[4] + sg[5] for sg in segs))
                if mg is None:
                    ei = nxt('EA', 4)
                    et, er = EA[ei], res('EA_%d' % ei)
                    rds, pvsegs = [br], segs
                elif mg == 'E':
                    ei = nxt('EBm%d' % n, 2)
                    et, er = EBm[n][ei], res('EBm%d_%d' % (n, ei))
                    rds, pvsegs = [br], segs
                else:
                    ei = nxt('EBm%d' % n, 2)
                    et, er = EBm[n][ei], res('EBm%d_%d' % (n, ei))
                    rds = [br, res('EBm%d_%d_sink' % (n, ei))]
                    pvsegs = [(None, None, mg[0], mg[1], 0, 81)]
                op('act', 'activation', reads=rds, writes=[er], out=et[lo:hi, 0:ncol], in_=bk[lo:hi, 0:ncol], func=AF.Exp, scale=0.125)
                ebufs.append((et, er, pvsegs, mg is not None and mg != 'E'))
            eb_all.append(ebufs)
        return eb_all

    def attn_pv(n_q, q0, eb_all, ores):
        ncol = 8 * n_q
        for n in range(2):
            ebufs = eb_all[n]
            pn, prn = bank_get(att=True)
            pd, prd = bank_get(att=True)
            has_merged = any(e[3] for e in ebufs)
            flat = [(et, er, sg) for et, er, segs, mgd in ebufs for sg in segs]
            grp = lambda sg: ('A' if sg[5] > 64 else ('hi' if sg[4] >= 64 else 'lo'))
            his = [f for f in flat if grp(f[2]) == 'hi']
            As = [f for f in flat if grp(f[2]) == 'A']
            los = [f for f in flat if grp(f[2]) == 'lo']
            num_order = his + As + los
            den_order = los + ([] if has_merged else ['sink']) + As + his
            k = 0
            for et, er, (kt_ap, kres, v_ap, vres, rb, nk) in num_order:
                mm(pn[:, 0:ncol].rearrange('p (g q) -> p g q', g=8), v_ap[rb:rb + nk, n * 128:(n + 1) * 128], et[rb:rb + nk, 0:ncol].rearrange('p (g q) -> p g q', g=8), k == 0, k == len(num_order) - 1, (vres if isinstance(vres, list) else [vres]) + [er], [prn], tp=(rb, 0))
                k += 1
            k = 0
            for item in den_order:
                if item == 'sink':
                    mm(pd[:, 0:ncol].rearrange('p (g q) -> p g q', g=8), ones[0:1, 0:128], esrow[0:1, n, :].rearrange('o (g q) -> o g q', g=8)[:, :, 0:n_q], k == 0, k == len(den_order) - 1, [res('ones'), res('esrow')], [prd], tp=(0, 0))
                else:
                    et, er, (kt_ap, kres, v_ap, vres, rb, nk) = item
                    mm(pd[:, 0:ncol].rearrange('p (g q) -> p g q', g=8), ones[rb:rb + nk, 0:128], et[rb:rb + nk, 0:ncol].rearrange('p (g q) -> p g q', g=8), k == 0, k == len(den_order) - 1, [res('ones'), er], [prd], tp=(rb, 0))
                k += 1
            gi = nxt('ga', 2)
            nsb, nsr = ga[gi], res('ga%d' % gi)
            rd, rr = gp[gi], res('gp%d' % gi)
            op('act', 'activation', reads=[prd], writes=[rr], out=rd[:, 0:ncol], in_=pd[:, 0:ncol], func=AF.Ln)
            op('act', 'activation', reads=[rr], writes=[rr], out=rd[:, 0:ncol], in_=rd[:, 0:ncol], func=AF.Exp, scale=-1.0)
            op('dve', 'tensor_copy', reads=[prn], writes=[nsr], out=nsb[:, 0:ncol], in_=pn[:, 0:ncol])
            for par in range(2):
                ps_ = slice(par * 64, (par + 1) * 64)
                nv = nsb[ps_, 0:ncol].rearrange('p (j two q) -> p j two q', two=2, q=n_q)[:, :, par, :]
                rv = rd[ps_, 0:ncol].rearrange('p (j two q) -> p j two q', two=2, q=n_q)[:, :, par, :]
                op('pool', 'tensor_tensor', reads=[nsr, rr], writes=[ores], out=QO[ps_, n * 4:(n + 1) * 4, q0:q0 + n_q], in0=nv, in1=rv, op=ALU.mult)

    def attention(n_q, q0, qres, key_banks, ores):
        attn_pv(n_q, q0, attn_scores(n_q, q0, qres, key_banks), ores)

    def pool_windows(P, Pres, L, g):
        i = nxt('ptAB', 1)
        tA, tB = (ptA[i], ptB[i])
        rA, rB = (res('ptA%d' % i), res('ptB%d' % i))
        op('pool', 'tensor_tensor', reads=[Pres], writes=[rA], out=tA[:, 1:L], in0=P[:, 1:L], in1=P[:, 0:L - 1], op=ALU.add)
        if g == 0:
            return (tA, rA)
        op('pool', 'tensor_tensor', reads=[rA], writes=[rB], out=tB[:, 3:L], in0=tA[:, 3:L], in1=tA[:, 1:L - 2], op=ALU.add)
        if g == 1:
            return (tB, rB)
        op('pool', 'tensor_tensor', reads=[rB], writes=[rA], out=tA[:, 7:L], in0=tB[:, 7:L], in1=tB[:, 3:L - 4], op=ALU.add)
        if g == 2:
            return (tA, rA)
        op('pool', 'tensor_tensor', reads=[rA], writes=[rB], out=tB[:, 15:L], in0=tA[:, 15:L], in1=tA[:, 7:L - 8], op=ALU.add)
        return (tB, rB)

    def run_tile(tile, xl, next_x_cb, hoist_cb, pre_normed, split=False):
        special = tile < 0
        NT = 32 if special else TT
        nsub = len(xl)
        hres = [res('hT%d' % s) for s in range(nsub)]
        h2x, h2n = (h2Ts, 'h2Ts') if special else (h2T, 'h2T')
        if not pre_normed:
            norm_T(xl, g1col, res('g1col'))
            rope_tables(tile)
        if _DBG.get('stage') == 'A':
            return
        thirds = ((0, 512), (512, 512), (1024, 256))
        kt_jobs = []
        for ti in (2, 0, 1):
            c0, ncol = thirds[ti]
            bks = [bank_get(att=(ti < 2)) for _ in range(nsub)]
            if ti == 2 and not special:
                bkx, brx = bank_get()
            for kh in range(2):
                wv, wr = w_get('qkv')
                for kk in range(4):
                    k = kh * 4 + kk
                    for s, (xt, xr, npr) in enumerate(xl):
                        mm(bks[s][0][0:npr, 0:ncol], hT[:, k, s * 128:s * 128 + npr], wv[:, kk, :], k == 0, k == 7, [hres[s], wr], [bks[s][1]])
                if ti != 2:
                    w_issue()
            if ti == 2 and not special:
                i1 = ws['used'] - 2
                for s, (xt, xr, npr) in enumerate(xl):
                    for kh in range(2):
                        sl = (i1 + kh) % NSLOT
                        wv2 = wslot[sl][:, 0:4 * 256].rearrange('p (k c) -> p k c', k=4)
                        for kk in range(4):
                            k = kh * 4 + kk
                            mm(bkx[0:64, s * 128:(s + 1) * 128], hT[:, k, s * 128 + 64:s * 128 + 128], wv2[:, kk, 128:256], k == 0, k == 7, [hres[s], res('wslot%d' % sl)], [brx])
                w_issue()
                w_issue()
            if ti == 2 and special:
                bkm, brm = bank_get()
                i1 = ws['used'] - 2
                for kh in range(2):
                    sl = (i1 + kh) % NSLOT
                    wv2 = wslot[sl][:, 0:4 * 256].rearrange('p (k c) -> p k c', k=4)
                    for kk in range(4):
                        k = kh * 4 + kk
                        mm(bkm[64:80, 0:128], hT[:, k, 0:16], wv2[:, kk, 128:256], k == 0, k == 7, [hres[0], res('wslot%d' % sl)], [brm], tp=(0, 64))
                        mm(bkm[0:16, 0:128], hT[:, k, 16:32], wv2[:, kk, 128:256], k == 0, k == 7, [hres[0], res('wslot%d' % sl)], [brm])
                op('act', 'copy', reads=[brm], writes=[res('VmHi')], out=vdup_out(VmHi[64:80, :]), in_=vdup_in(bkm[64:80, 0:128], 16))
                for r in range(9):
                    op('pool', 'tensor_copy', reads=[res('VmHi')], writes=[res('VBm%d' % r)], out=VB[64:80, r, :], in_=VmHi[64:80, :])
                op('act', 'copy', reads=[brm], writes=[res('Vs')], out=vdup_out(Vs[0:16, :]), in_=vdup_in(bkm[0:16, 0:128], 16))
                w_issue()
                w_issue()
            for s, (xt, xr, npr) in enumerate(xl):
                bk, br = bks[s]
                if ti < 2:
                    n = ti
                    outv = qb[0:npr, s, :].rearrange('p (g n d) -> p n g d', n=2, d=64)[:, n]
                    rope_bank(bk, br, s, npr, 8, 0, outv, [res('aT%d' % (2 * s)), res('aT%d' % (2 * s + 1))], 'pool')
                else:
                    i = nxt('kf', 2)
                    kft, kbtt, vft = (kf[i], kbt[s], vf[i])
                    kfr, kbr, vfr = (res('kf%d' % i), res('kbt%d' % s), res('vf%d' % i))
                    rope_bank(bk, br, s, npr, 2, 0, kft[0:npr, :].rearrange('p (h e) -> p h e', e=64), kfr, 'pool')
                    op('act', 'copy', reads=[kfr], writes=[kbr], out=kbtt[0:npr, :], in_=kft[0:npr, :])
                    if special:
                        op('act', 'copy', reads=[br], writes=[res('VmLo')], out=vdup_out(VmLo[0:16, :]), in_=vdup_in(bk[0:16, 128:256], 16))
                        op('act', 'copy', reads=[br], writes=[vfr], out=vft[0:32, :], in_=bk[0:32, 128:256])
                        dma('sp', o_metak, kft[0:16, :], reads=[kfr], out=True)
                        dma('sp', o_sk, kft[16:32, :], reads=[kfr], out=True)
                        dma('sp', o_metav, vft[0:16, :], reads=[vfr], out=True)
                        dma('sp', o_sv, vft[16:32, :], reads=[vfr], out=True)
                    else:
                        gsub = tile * 4 + s
                        vslot = gsub % 8
                        op('act', 'copy', reads=[br], writes=[res('VR%d' % vslot)], out=vdup_out(VR[:, vslot, :]), in_=vdup_in(bk[:, 128:256], 128))
                        se, so = vb_slot(tile * 8 + 2 * s), vb_slot(tile * 8 + 2 * s + 1)
                        op('act', 'copy', reads=[br], writes=[res('VB%d' % se)], out=vdup_out(VB[0:64, se, :]), in_=vdup_in(bk[0:64, 128:256], 64))
                        op('act', 'copy', reads=[brx], writes=[res('VB%d' % so)], out=vdup_out(VB[0:64, so, :]), in_=vdup_in(bkx[0:64, s * 128:(s + 1) * 128], 64))
                        if tile == NTILE - 1 and s == 3:
                            op('act', 'copy', reads=[br], writes=[vfr], out=vft[:, :], in_=bk[:, 128:256])
                            dma('sp', o_swak, kft[:, :], reads=[kfr], out=True)
                            dma('sp', o_swav, vft[:, :], reads=[vfr], out=True)
                    kt_jobs.append((s, npr, kbtt, kbr))
        yield 'projB_done'
        if special:
            precast(s_up, w_up, D, 'sc_up')
            precast(s_down, w_down, DFF, 'sc_down')
        pun = [w_get('p'), w_get('p')]
        pbanks = []
        for g in range(4):
            bk, br = bank_get()
            wv, wr = pun[g // 2]
            for k in range(8):
                mm(bk[:, 0:NT], wv[:, k, g % 2 * 128:g % 2 * 128 + 128], hT[:, k, 0:NT], k == 0, k == 7, hres + [wr], [br])
            pbanks.append((bk, br))
        if special or tile == NTILE - 1:
            s_last = nsub - 1
            npr = xl[s_last][2]
            bk, br = bank_get()
            for half in range(2):
                wv, wr = pun[half]
                for k in range(8):
                    mm(bk[0:npr, half * 256:(half + 1) * 256], hT[:, k, s_last * 128:s_last * 128 + npr], wv[:, k, :], k == 0, k == 7, [hres[s_last], wr], [br])
            op('act', 'copy', reads=[br], writes=[res('ptok')], out=ptok[0:npr, :], in_=bk[0:npr, :])
            if special:
                dma('sp', o_spool, ptok[17:32, :], reads=[res('ptok')], out=True)
            else:
                dma('sp', o_ppool, ptok[113:128, :], reads=[res('ptok')], out=True)
        w_issue()
        w_issue()
        for g in range(4):
            bk, br = pbanks[g]
            w = POOLW[g]
            if special:
                op('act', 'copy', reads=[br], writes=[res('pS0_%d' % g)], out=pS[0][:, g, 15:31], in_=bk[:, 0:16])
                op('act', 'copy', reads=[br], writes=[res('pS1_%d' % g)], out=pS[1][:, g, 15:31], in_=bk[:, 16:32])
                fin, fr = pool_windows(pS[0][:, g, :], res('pS0_%d' % g), 31, g)
                op('dve', 'tensor_tensor', reads=[fr, res('invcnt')], writes=[res('ptmp')], out=ptmp[:, 0:16], in0=fin[:, 15:31], in1=invcnt[:, g, :], op=ALU.mult)
                op('dve', 'tensor_tensor', reads=[res('ptmp'), res('pS0_%d' % g)], writes=[res('pmT%d' % g)], out=pmT[:, g, 0:16], in0=ptmp[:, 0:16], in1=pS[0][:, g, 15:31], op=ALU.subtract)
                fin, fr = pool_windows(pS[1][:, g, :], res('pS1_%d' % g), 31, g)
                op('dve', 'scalar_tensor_tensor', reads=[fr, res('pS1_%d' % g)], writes=[res('pmT%d' % g)], out=pmT[:, g, 16:32], in0=fin[:, 15:31], scalar=1.0 / w, in1=pS[1][:, g, 15:31], op0=ALU.mult, op1=ALU.subtract)
            else:
                op('act', 'copy', reads=[br], writes=[res('pTc%d' % g)], out=pT[:, g, 15:15 + TT], in_=bk[:, 0:TT])
                fin, fr = pool_windows(pT[:, g, :], res('pTc%d' % g), 15 + TT, g)
                op('dve', 'scalar_tensor_tensor', reads=[fr, res('pTc%d' % g)], writes=[res('pmT%d' % g)], out=pmT[:, g, 0:TT], in0=fin[:, 15:15 + TT], scalar=1.0 / w, in1=pT[:, g, 15:15 + TT], op0=ALU.mult, op1=ALU.subtract)
        if special:
            op('pool', 'tensor_copy', reads=[res('pS0_%d' % g) for g in range(4)], writes=[res('pTc%d' % g) for g in range(4)], out=pT[:, :, 0:15], in_=pS[0][:, :, 16:31])
        elif tile < NTILE - 1:
            op('pool', 'tensor_copy', reads=[res('pTc%d' % g) for g in range(4)], writes=[res('pTc%d' % g) for g in range(4)], out=pT[:, :, 0:15], in_=pT[:, :, TT:TT + 15])
        for (s, npr, kbtt, kbr) in kt_jobs:
            tb, tr = trb_get()
            op('pe', 'transpose', reads=[kbr, res('ident')], writes=[tr], _a=(tb[:, 0, 0:npr], kbtt[0:npr, :], ident[0:npr, 0:npr]))
            if special:
                op('act', 'copy', reads=[tr], writes=[res('KTmeta')], out=KT[:, 0:16], in_=tb[:, 0, 0:16])
                op('act', 'copy', reads=[tr], writes=[res('KTs')], out=KTs[:, 0:16], in_=tb[:, 0, 16:32])
            else:
                gt = (tile * TT + s * 128) % 1024
                op('act', 'copy', reads=[tr], writes=[res('KT%d' % (gt // 128))], out=KT[:, 16 + gt:16 + gt + 128], in_=tb[:, 0, :])
        qchunks = [res('QO%d' % c) for c in range(max(1, NT // 64))]
        for s, (xt, xr, npr) in enumerate(xl):
            tb, tr = trb_get()
            for g in range(8):
                op('pe', 'transpose', reads=[res('aT%d' % (2 * s)), res('aT%d' % (2 * s + 1)), res('ident')], writes=[tr], _a=(tb[:, g, 0:npr], qb[0:npr, s, g * 128:(g + 1) * 128], ident[0:npr, 0:npr]))
            wr_ = [qchunks[0]] if special else [qchunks[2 * s], qchunks[2 * s + 1]]
            op('act', 'copy', reads=[tr], writes=wr_, out=QO[:, :, s * 128:s * 128 + npr], in_=tb[:, :, 0:npr])
        if special:
            attention(16, 0, qchunks[0], [[(KT[:, 0:16], res('KTmeta'), VmLo, res('VmLo'), 0, 16)]], qchunks[0])
            if _DBG.get('stage') == 'D1':
                return
            _kbA = [(KTcs[:, :], res('KTcs'), Vcs, res('Vcs'), 0, 128)]
            _kbB1 = (KTs[:, 0:16], res('KTs'), Vs, res('Vs'), 0, 16)
            _kbB2 = (KTcm[:, 0:16], res('KTcm'), Vcm, res('Vcm'), 64, 16)
            _st = _DBG.get('stage')
            if _st == 'D2a':
                attention(16, 16, qchunks[0], [_kbA], qchunks[0]); return
            if _st == 'D2b':
                attention(16, 16, qchunks[0], [[_kbB1]], qchunks[0]); return
            if _st == 'D2c':
                attention(16, 16, qchunks[0], [[_kbB1, _kbB2]], qchunks[0]); return
            if _st == 'D2d':
                attention(16, 16, qchunks[0], [[_kbB2]], qchunks[0]); return
            attention(16, 16, qchunks[0], [_kbA, [_kbB1, _kbB2]], qchunks[0])
        else:
            att_pending = None
            for cl in range(8):
                c = tile * 8 + cl

                def ktc(j):
                    col = 16 + j * 64 % 1024
                    return (col, res('KT%d' % (j * 64 % 1024 // 128)))

                def vsl(j):
                    sl = j // 2 % 8
                    return (VR[:, sl, :], res('VR%d' % sl))
                kb_list = []

                def merged(j):
                    col, kr = ktc(j)
                    return ('M', [(KT[:, col:col + 64], kr, None, None, 0, 64), (KT[:, 0:16], res('KTmeta'), None, None, 64, 16)],
                            VB[:, vb_slot(j), :], [res('VB%d' % vb_slot(j)), res('VBm%d' % vb_slot(j))])
                if c % 2 == 0:
                    if c >= 2:
                        col, kr = ktc(c - 2)
                        va, vr = vsl(c - 2)
                        kb_list.append([(KT[:, col:col + 128], kr, va, vr, 0, 128)])
                    kb_list.append(merged(c))
                else:
                    col, kr = ktc(c - 1)
                    va, vr = vsl(c - 1)
                    kb_list.append([(KT[:, col:col + 128], kr, va, vr, 0, 128)])
                    if c >= 3:
                        kb_list.append(merged(c - 2))
                    else:
                        kb_list.append(('E', [(KT[:, 0:16], res('KTmeta'), VmHi, res('VmHi'), 64, 16)]))
                sc = attn_scores(64, cl * 64, qchunks[cl], kb_list)
                if att_pending is not None:
                    attn_pv(*att_pending)
                att_pending = (64, cl * 64, sc, qchunks[cl])
            attn_pv(*att_pending)
        wv, wr = w_get('grp')
        for g in range(4):
            bk, br = bank_get()
            mm(bk[:, 0:NT], wv[:, g, :], pmT[:, g, 0:NT], True, True, [res('pmT%d' % g), wr], [br])
            op('act', 'activation', reads=[br, res('pscol')], writes=[res('poolT%d' % g)], out=poolT[:, g, 0:NT], in_=bk[:, 0:NT], func=AF.Copy, scale=pscol[:, g:g + 1])
        w_issue()
        for jp in range(4):
            wao, rao = w_get('ao')
            wpo, rpo = w_get('po')
            wga, rga = w_get('ga')
            wgp, rgp = w_get('gp')
            for jj in range(2):
                jc = jp * 2 + jj
                cs = slice(jj * 128, jj * 128 + 128)
                bA, rA = bank_get()
                for k in range(8):
                    mm(bA[:, 0:NT], wao[:, k, cs], QO[:, k, 0:NT], k == 0, k == 7, qchunks + [rao], [rA])
                bP, rP = bank_get()
                for k in range(4):
                    mm(bP[:, 0:NT], wpo[:, k, cs], poolT[:, k, 0:NT], k == 0, k == 3, [res('poolT%d' % k), rpo], [rP])
                bGa, rGa = bank_get()
                for k in range(8):
                    mm(bGa[:, 0:NT], wga[:, k, cs], hT[:, k, 0:NT], k == 0, k == 7, hres + [rga], [rGa])
                bGp, rGp = bank_get()
                for k in range(8):
                    mm(bGp[:, 0:NT], wgp[:, k, cs], hT[:, k, 0:NT], k == 0, k == 7, hres + [rgp], [rGp])
                gi = nxt('ga', 2)
                gat, gpt = (ga[gi], gp[gi])
                gar, gpr = (res('ga%d' % gi), res('gp%d' % gi))
                op('act', 'activation', reads=[rGa, res('bgcol')], writes=[gar], out=gat[:, 0:NT], in_=bGa[:, 0:NT], func=AF.Sigmoid, bias=bgcol[:, jc:jc + 1])
                op('act', 'activation', reads=[rGp, res('bgcol')], writes=[gpr], out=gpt[:, 0:NT], in_=bGp[:, 0:NT], func=AF.Sigmoid, bias=bgcol[:, 8 + jc:9 + jc])
                op('dve', 'tensor_tensor', reads=[gar, rA], writes=[gar], out=gat[:, 0:NT], in0=gat[:, 0:NT], in1=bA[:, 0:NT], op=ALU.mult)
                op('dve', 'tensor_tensor', reads=[gpr, rP], writes=[gpr], out=gpt[:, 0:NT], in0=gpt[:, 0:NT], in1=bP[:, 0:NT], op=ALU.mult)
                op('pool', 'tensor_tensor', reads=[gar, gpr], writes=[res('mixT%d' % jc)], out=mixT[:, jc, 0:NT], in0=gat[:, 0:NT], in1=gpt[:, 0:NT], op=ALU.add)
            for _ in range(4):
                w_issue()
        mres = [res('mixT%d' % j) for j in range(8)]
        wu = [[w_get('out'), w_get('out')], [w_get('out'), w_get('out')]]
        if not (split and tile == 0):
            next_x_cb()
        pend = None
        for s, (xt, xr, npr) in enumerate(xl):
            for half in range(2):
                bk, br = bank_get()
                for k in range(8):
                    wv, wr = wu[half][k // 4]
                    mm(bk[0:npr, :], mixT[:, k, s * 128:s * 128 + npr], wv[:, k % 4, :], k == 0, k == 7, [mres[k], wr], [br])
                hs = slice(half * 512, half * 512 + 512)
                op('dve', 'tensor_tensor', reads=[xr, br], writes=[xr], out=xt[0:npr, hs], in0=xt[0:npr, hs], in1=bk[0:npr, :], op=ALU.add)
            hbt, hr = norm_pre(xt, xr, npr)
            if pend is not None:
                norm_post(*pend)
            pend = (s, npr, hbt, hr, g2col, res('g2col'), h2x, h2n)
        norm_post(*pend)
        for _ in range(4):
            w_issue()
        yield 'mix_done'
        if split and tile == 0:
            next_x_cb()
        h2res = [res('%s%d' % (h2n, s)) for s in range(nsub)]
        nseg = 2 if special else 1
        sl_ = NT // nseg
        for ip in range(11):
            wg, rg = w_get('upg')
            wvv, rv = w_get('upv')
            for jj in range(2):
                i = ip * 2 + jj
                cs = slice(jj * 128, jj * 128 + 128)
                cbufs = []
                for which, (wt, wr_) in enumerate(((wg, rg), (wvv, rv))):
                    ch = i + which * 22
                    bk, br = bank_get()
                    for k in range(8):
                        mm(bk[:, 0:NT], wt[:, k, cs], h2x[:, k, 0:NT], k == 0, k == 7, h2res + [wr_], [br])
                    ui = nxt('u_sb', 4)
                    ut, ur = (u_sb[ui], res('u_sb%d' % ui))
                    ci = nxt('c_sb', 4)
                    ct, cr = (c_sb[ci], res('c_sb%d' % ci))
                    uv = ut[:, 0:nseg * (sl_ + 2)].rearrange('p (k c) -> p k c', k=nseg)
                    bv = bk[:, 0:NT].rearrange('p (k c) -> p k c', k=nseg)
                    cv = ct[:, 0:NT].rearrange('p (k c) -> p k c', k=nseg)
                    csr = res('cs_p%d' % ch)
                    if special:
                        op('pool', 'tensor_copy', reads=[res('zero2')], writes=[ur], out=uv[:, 0, 0:2], in_=zero2[:, :])
                        op('pool', 'tensor_copy', reads=[res('cs_s%d' % ch)], writes=[ur], out=uv[:, 1, 0:2], in_=cs_s[:, ch, :])
                    else:
                        op('pool', 'tensor_copy', reads=[csr], writes=[ur], out=uv[:, 0, 0:2], in_=cs_p[:, ch, :])
                    op('act', 'copy', reads=[br], writes=[ur], out=uv[:, :, 2:2 + sl_], in_=bv)
                    if special:
                        op('act', 'copy', reads=[br], writes=[csr], out=cs_p[:, ch, :], in_=bk[:, 14:16])
                        op('act', 'copy', reads=[br], writes=[res('cs_s%d' % ch)], out=cs_s[:, ch, :], in_=bk[:, 30:32])
                    else:
                        op('act', 'copy', reads=[br], writes=[csr], out=cs_p[:, ch, :], in_=bk[:, TT - 2:TT])
                    op('act', 'activation', reads=[br, res('cwcol'), res('cbcol')], writes=[cr], out=cv, in_=bv, func=AF.Identity, scale=cwcol[:, 2, ch:ch + 1], bias=cbcol[:, ch:ch + 1])
                    op('dve', 'scalar_tensor_tensor', reads=[ur, cr, res('cwcol')], writes=[cr], out=cv, in0=uv[:, :, 1:1 + sl_], scalar=cwcol[:, 1, ch:ch + 1], in1=cv, op0=ALU.mult, op1=ALU.add)
                    op('dve', 'scalar_tensor_tensor', reads=[ur, cr, res('cwcol')], writes=[cr], out=cv, in0=uv[:, :, 0:sl_], scalar=cwcol[:, 0, ch:ch + 1], in1=cv, op0=ALU.mult, op1=ALU.add)
                    cbufs.append((ct, cr))
                (cg, cgr), (cvv, cvr) = cbufs
                op('act', 'activation', reads=[cgr], writes=[cgr], out=cg[:, 0:NT], in_=cg[:, 0:NT], func=AF.Silu)
                op('dve', 'tensor_tensor', reads=[cgr, cvr], writes=[res('aT%d' % i)], out=aT[:, i, 0:NT], in0=cg[:, 0:NT], in1=cvv[:, 0:NT], op=ALU.mult)
            w_issue()
            w_issue()
        hoist_cb(0)
        for half in range(2):
            bks = [bank_get() for _ in range(nsub)]
            for kq in range(6):
                wv, wr = w_get('down')
                for kk in range(min(4, 22 - kq * 4)):
                    i = kq * 4 + kk
                    for s, (xt, xr, npr) in enumerate(xl):
                        mm(bks[s][0][0:npr, :], aT[:, i, s * 128:s * 128 + npr], wv[:, kk, :], i == 0, i == 21, [res('aT%d' % i), wr], [bks[s][1]])
                w_issue()
            for s, (xt, xr, npr) in enumerate(xl):
                bk, br = bks[s]
                hs = slice(half * 512, half * 512 + 512)
                op('dve', 'tensor_tensor', reads=[xr, br], writes=[xr], out=xt[0:npr, hs], in0=xt[0:npr, hs], in1=bk[0:npr, :], op=ALU.add)
            hoist_cb(1 + half)
        yield 'pre_tail'
        junk = (pmT[:, 0:2, :].rearrange('p a c -> p (a c)'), [res('pmT0'), res('pmT1')])
        for s, (xt, xr, npr) in enumerate(xl):
            rstd, sr, hbt, hr = rms_rstd(xt, xr, npr, junk)
            op('dve', 'scalar_tensor_tensor', reads=[xr, sr, res('g3b')], writes=[xr], out=xt[0:npr, :], in0=xt[0:npr, :], scalar=rstd, in1=g3b[0:npr, :], op0=ALU.mult, op1=ALU.mult)
            if special:
                dma('sp', y_s, xt[16:32, :], reads=[xr], out=True)
            else:
                r0 = tile * TT + s * 128
                dma('sp', y_p[r0:r0 + 128, :], xt[:, :], reads=[xr], out=True)
        def store_conv_state(cs, csname, dst):
            creads = [res('%s%d' % (csname, c)) for c in range(NCH)]
            bk, br = bank_get()
            for t2 in range(2):
                op('pe', 'transpose', creads + [res('identF')], [br], _a=(bk[0:NCH, t2 * 128:(t2 + 1) * 128], cs[:, :, t2], identF[:, :]))
            op('act', 'copy', reads=[br], writes=[res('ostg')], out=ostg[0:NCH, :, :], in_=bk[0:NCH, 0:256].rearrange('p (t c) -> p t c', t=2))
            for t2 in range(2):
                dma('sp', dst[t2, :].rearrange('(c p) -> c p', p=128), ostg[0:NCH, t2, :], reads=[res('ostg')], out=True)
        if special:
            store_conv_state(cs_s, 'cs_s', o_sconv)
        if tile == NTILE - 1:
            store_conv_state(cs_p, 'cs_p', o_pconv)
    pending = {}
    order = [-1] + list(range(NTILE))
    if _DBG.get('stage') == 'consts':
        order = []

    def mk_cb(nxt_tile):
        def cb():
            if nxt_tile is None:
                return
            pending['x'] = [load_x_sub(nxt_tile, s) for s in range(4)]
        return cb

    def mk_hoist(nxt_tile):
        def hoist(phase):
            if nxt_tile is None or _DBG.get('stage'):
                return
            nx = pending['x']
            args = lambda s_, pr: (s_, nx[s_][2], pr[0], pr[1], g1col, res('g1col'), hT, 'hT')
            if phase == 0:
                pending['pre01'] = [norm_pre(*nx[0]), norm_pre(*nx[1])]
                rope_tables(nxt_tile)
            elif phase == 1:
                p0, p1 = pending.pop('pre01')
                norm_post(*args(0, p0))
                p2 = norm_pre(*nx[2])
                norm_post(*args(1, p1))
                p3 = norm_pre(*nx[3])
                pending['pre23'] = [p2, p3]
            else:
                p2, p3 = pending.pop('pre23')
                norm_post(*args(2, p2))
                norm_post(*args(3, p3))
                pending['pre'] = True
        return hoist

    def drain(g):
        for _ in g:
            pass

    def advance(g, label):
        for v in g:
            if v == label:
                return

    xl = [load_x_sub(-1, 0)] if order else None
    if len(order) >= 2 and not _DBG.get('stage'):
        gS = run_tile(-1, xl, mk_cb(0), lambda phase: None, False, split=True)
        advance(gS, 'mix_done')
        xl0 = pending['x']
        nxt1 = order[2] if len(order) > 2 else None
        g0 = run_tile(0, xl0, mk_cb(nxt1), mk_hoist(nxt1), False, split=True)
        advance(g0, 'mix_done')
        drain(gS)
        advance(g0, 'pre_tail')
        prev = g0
        for idx in range(2, len(order)):
            tile = order[idx]
            nxt_tile = order[idx + 1] if idx + 1 < len(order) else None
            pre = pending.pop('pre', False)
            g = run_tile(tile, pending['x'], mk_cb(nxt_tile), mk_hoist(nxt_tile), pre)
            advance(g, 'projB_done')
            drain(prev)
            advance(g, 'pre_tail')
            prev = g
        drain(prev)
    else:
        for idx in range(len(order)):
            tile = order[idx]
            nxt_tile = order[idx + 1] if idx + 1 < len(order) else None
            pre = pending.pop('pre', False)
            drain(run_tile(tile, xl, mk_cb(nxt_tile), mk_hoist(nxt_tile), pre))
            if nxt_tile is not None:
                xl = pending['x']
    assert _DBG.get('stage') or ws['used'] == len(all_units), (ws['used'], len(all_units))
    S.finish()
    S.emit()
    return nc
_CACHE = {}
_DBG = {}

def kernel(x_prompt, x_sample, cache_swa_k, cache_swa_v, cache_meta_k, cache_meta_v, state_pool, state_conv, meta_tokens, g_norm_mix, w_in, b_gate, sinks, w_attn_o, w_pool_grp, pool_scale, w_pool_o, w_out, g_norm_ffn, w_up, conv_w, conv_b, w_down, g_norm_final):
    if 'nc' not in _CACHE:
        _CACHE['nc'] = build()
    nc = _CACHE['nc']
    f = lambda a: np.ascontiguousarray(np.asarray(a, dtype=np.float32))
    half = 32
    inv32 = (10000.0 ** (-np.arange(half, dtype=np.float64) / half)).astype(np.float32)
    ropec = np.concatenate([inv32, np.zeros(half, np.float32)])
    shared = {'meta': f(meta_tokens), 'g1': f(g_norm_mix), 'w_in': f(w_in), 'b_gate': f(b_gate), 'sinks': f(sinks), 'w_ao': f(w_attn_o), 'w_grp': f(w_pool_grp).reshape(512, 128), 'pscale': f(pool_scale), 'w_po': f(w_pool_o), 'w_out': f(w_out), 'g2': f(g_norm_ffn), 'w_up': f(w_up), 'conv_w': f(conv_w), 'conv_b': f(conv_b), 'w_down': f(w_down), 'g3': f(g_norm_final), 'ropec': ropec}
    xp_, xs_ = (f(x_prompt), f(x_sample))
    ck, cv, mk, mv = (f(cache_swa_k), f(cache_swa_v), f(cache_meta_k), f(cache_meta_v))
    sp_, sc_ = (f(state_pool), f(state_conv))
    in_maps = []
    for b in range(8):
        m = dict(shared)
        m.update({'xp': xp_[b], 'xs': xs_[b], 'cswak': ck[b].reshape(128, 128), 'cswav': cv[b].reshape(128, 128), 'cmetak': mk[b].reshape(16, 128), 'cmetav': mv[b].reshape(16, 128), 'spool': sp_[b], 'sconv': sc_[b]})
        in_maps.append(m)
    res = run_bass_kernel_spmd(nc, in_maps, core_ids=list(range(8)))
    rs = res.results
    st = lambda k, shp: np.stack([np.asarray(r[k], dtype=np.float32).reshape(shp) for r in rs], 0)
    return (st('y_p', (SEQ, D)), st('y_s', (DEC, D)), st('o_swak', (128, 2, 64)), st('o_swav', (128, 2, 64)), st('o_metak', (16, 2, 64)), st('o_metav', (16, 2, 64)), st('o_ppool', (15, 512)), st('o_pconv', (2, 2 * DFF)), st('o_sk', (16, 2, 64)), st('o_sv', (16, 2, 64)), st('o_spool', (15, 512)), st('o_sconv', (2, 2 * DFF)))
```
